# Optimizing a Trainium2 kernel written in Bass

```python
import math
import jax, jax.numpy as jnp
from jax import lax
import numpy as np

D_MODEL = 1024
BATCH = 4
SEQ = 8192
DEPTH = 1

N_ATT_HEADS = 8
ATT_HEAD_DIM = 64
ATT_V_DIM = 2 * ATT_HEAD_DIM
D_ATT_QK = N_ATT_HEADS * 2 * ATT_HEAD_DIM
D_ATT_V = N_ATT_HEADS * ATT_V_DIM
Q_BLOCK = 128
D_RNN = D_MODEL
N_RNN_BLOCKS = 8
RNN_BLOCK = D_RNN // N_RNN_BLOCKS
CONV_WIDTH = 4
LRU_C = 8.0
N_BRANCH = 2
D_FF = 4 * D_MODEL
D_IN = 2 * D_ATT_QK + D_ATT_V + 2 * D_RNN + N_BRANCH * D_MODEL
EPS = 1e-6

kernel_name = 'hybrid_diffattn_rglru_gated_block'


def _rmsnorm(x, g):
    x32 = x.astype(jnp.float32)
    y = x32 * lax.rsqrt(jnp.mean(jnp.square(x32), axis=-1, keepdims=True) + EPS)
    return (y * g.astype(jnp.float32)).astype(x.dtype)


def _alibi_slopes(n):
    return 2.0 ** (-8.0 * jnp.arange(1, n + 1, dtype=jnp.float32) / n)


def _diff_attention(q, k, v, lam):
    b, s = q.shape[:2]
    nblk = s // Q_BLOCK
    slopes = _alibi_slopes(N_ATT_HEADS)
    kpos = jnp.arange(s)
    qb = q.reshape(b, nblk, Q_BLOCK, N_ATT_HEADS, 2, ATT_HEAD_DIM).transpose(1, 0, 2, 3, 4, 5)

    def one_block(args):
        i, qi = args
        qpos = i * Q_BLOCK + jnp.arange(Q_BLOCK)
        dist = (qpos[:, None] - kpos[None, :]).astype(jnp.float32)
        bias = jnp.where(dist >= 0, -slopes[:, None, None] * dist, -jnp.inf)
        scores = jnp.einsum('bqhcd,bkhcd->bhcqk', qi, k, preferred_element_type=jnp.float32)
        p = jax.nn.softmax(scores + bias[None, :, None], axis=-1)
        attn = (p[:, :, 0] - lam * p[:, :, 1]).astype(v.dtype)
        return jnp.einsum('bhqk,bkhe->bqhe', attn, v)

    out = lax.map(one_block, (jnp.arange(nblk), qb))
    return out.transpose(1, 0, 2, 3, 4).reshape(b, s, N_ATT_HEADS, ATT_V_DIM)


def _rglru_branch(xb, gb, conv_w, conv_b, w_r, b_r, w_i, b_i, lru_lambda):
    b, s, _ = xb.shape
    xp = jnp.pad(xb, ((0, 0), (CONV_WIDTH - 1, 0), (0, 0)))
    xc = conv_b + xp[:, 0:s] * conv_w[0]
    for j in range(1, CONV_WIDTH):
        xc = xc + xp[:, j:j + s] * conv_w[j]
    xr = xc.reshape(b, s, N_RNN_BLOCKS, RNN_BLOCK)
    r = jax.nn.sigmoid(jnp.einsum('bsnc,ncd->bsnd', xr, w_r).reshape(b, s, D_RNN) + b_r)
    i = jax.nn.sigmoid(jnp.einsum('bsnc,ncd->bsnd', xr, w_i).reshape(b, s, D_RNN) + b_i)
    log_a = -LRU_C * jax.nn.softplus(-lru_lambda.astype(jnp.float32)) * r.astype(jnp.float32)
    a = jnp.exp(log_a)
    mult = jnp.sqrt(-jnp.expm1(2.0 * log_a))
    pos = jnp.arange(s)[None, :, None]
    mult = jnp.where(pos == 0, 1.0, mult)
    u = mult * (i * xc).astype(jnp.float32)

    def combine(left, right):
        a1, b1 = left
        a2, b2 = right
        return a1 * a2, a2 * b1 + b2

    _, h = lax.associative_scan(combine, (a, u), axis=1)
    return h.astype(xb.dtype) * jax.nn.gelu(gb)


def setup_inputs(seed: int = 0) -> dict:
    key = jax.random.key(seed)
    ks = jax.random.split(key, 24)
    L = DEPTH
    nrm = lambda k, shp, sc: jax.random.normal(k, shp, jnp.float32) * sc
    u = jax.random.uniform(ks[11], (L, D_RNN), jnp.float32, 0.9, 0.999)
    a0 = u ** (1.0 / LRU_C)
    lru_lambda = jnp.log(a0) - jnp.log1p(-a0)
    return {
        'x': nrm(ks[0], (BATCH, SEQ, D_MODEL), 1.0),
        'w_in': nrm(ks[1], (L, D_MODEL, D_IN), D_MODEL ** -0.5),
        'b_gate': nrm(ks[2], (L, N_BRANCH * D_MODEL), 0.02),
        'g_mix': 1.0 + nrm(ks[3], (L, D_MODEL), 0.02),
        'lambda_q1': nrm(ks[4], (L, ATT_HEAD_DIM), 0.1),
        'lambda_k1': nrm(ks[5], (L, ATT_HEAD_DIM), 0.1),
        'lambda_q2': nrm(ks[6], (L, ATT_HEAD_DIM), 0.1),
        'lambda_k2': nrm(ks[7], (L, ATT_HEAD_DIM), 0.1),
        'subln_g': 1.0 + nrm(ks[8], (L, ATT_V_DIM), 0.02),
        'conv_w': nrm(ks[9], (L, CONV_WIDTH, D_RNN), CONV_WIDTH ** -0.5),
        'conv_b': nrm(ks[10], (L, D_RNN), 0.02),
        'w_r': nrm(ks[12], (L, N_RNN_BLOCKS, RNN_BLOCK, RNN_BLOCK), RNN_BLOCK ** -0.5),
        'b_r': nrm(ks[13], (L, D_RNN), 0.02),
        'w_i': nrm(ks[14], (L, N_RNN_BLOCKS, RNN_BLOCK, RNN_BLOCK), RNN_BLOCK ** -0.5),
        'b_i': nrm(ks[15], (L, D_RNN), 0.02),
        'lru_lambda': lru_lambda,
        'w_att_out': nrm(ks[16], (L, D_ATT_V, D_MODEL), D_ATT_V ** -0.5),
        'w_rnn_out': nrm(ks[17], (L, D_RNN, D_MODEL), D_RNN ** -0.5),
        'w_o': nrm(ks[18], (L, D_MODEL, D_MODEL), D_MODEL ** -0.5),
        'g_mlp': 1.0 + nrm(ks[19], (L, D_MODEL), 0.02),
        'w_ff1': nrm(ks[20], (L, D_MODEL, D_FF), D_MODEL ** -0.5),
        'w_ff2': nrm(ks[21], (L, D_FF, D_MODEL), D_FF ** -0.5),
        'g_final': 1.0 + nrm(ks[22], (D_MODEL,), 0.02),
    }


def reference(x, w_in, b_gate, g_mix, lambda_q1, lambda_k1, lambda_q2, lambda_k2, subln_g,
              conv_w, conv_b, w_r, b_r, w_i, b_i, lru_lambda, w_att_out, w_rnn_out, w_o,
              g_mlp, w_ff1, w_ff2, g_final):
    b, s, _ = x.shape
    splits = [D_ATT_QK, 2 * D_ATT_QK, 2 * D_ATT_QK + D_ATT_V,
              2 * D_ATT_QK + D_ATT_V + D_RNN, 2 * D_ATT_QK + D_ATT_V + 2 * D_RNN]
    for l in range(DEPTH):
        h = _rmsnorm(x, g_mix[l])
        z = h @ w_in[l]
        q, k, v, xb, gb, gates = jnp.split(z, splits, axis=-1)
        q = q.reshape(b, s, N_ATT_HEADS, 2, ATT_HEAD_DIM) * (ATT_HEAD_DIM ** -0.5)
        k = k.reshape(b, s, N_ATT_HEADS, 2, ATT_HEAD_DIM)
        v = v.reshape(b, s, N_ATT_HEADS, ATT_V_DIM)
        lam_init = 0.8 - 0.6 * math.exp(-0.3 * l)
        lam = (jnp.exp(jnp.sum(lambda_q1[l].astype(jnp.float32) * lambda_k1[l].astype(jnp.float32)))
               - jnp.exp(jnp.sum(lambda_q2[l].astype(jnp.float32) * lambda_k2[l].astype(jnp.float32)))
               + lam_init)
        att = _diff_attention(q, k, v, lam)
        att = _rmsnorm(att, subln_g[l]) * (1.0 - lam_init)
        y_att = att.reshape(b, s, D_ATT_V) @ w_att_out[l]
        y_rnn = _rglru_branch(xb, gb, conv_w[l], conv_b[l], w_r[l], b_r[l], w_i[l], b_i[l],
                              lru_lambda[l]) @ w_rnn_out[l]
        g = jax.nn.sigmoid(gates + b_gate[l]).reshape(b, s, N_BRANCH, D_MODEL)
        m = g[:, :, 0] * y_att + g[:, :, 1] * y_rnn
        x = x + m @ w_o[l]
        h2 = _rmsnorm(x, g_mlp[l])
        x = x + jnp.square(jax.nn.relu(h2 @ w_ff1[l])) @ w_ff2[l]
    return _rmsnorm(x, g_final)
```

```python
import contextlib
import numpy as np
import ml_dtypes
import concourse.bass as bass
import concourse.mybir as mybir
from concourse.bass_utils import run_bass_kernel_spmd

F32 = mybir.dt.float32
BF16 = mybir.dt.bfloat16
AF = mybir.ActivationFunctionType
ALU = mybir.AluOpType
AX = mybir.AxisListType

S = 8192
D = 1024
NOWN = 4096
T = 512
EPS = 1e-6
NPV = 361
NCST = 770
SUBW = [128, 256, 512, 512, 512, 512, 512, 512]
WINB = [-(-int(150 * 2 ** (h + 1)) // 128) for h in range(8)]
BIAS_BASE = []
_b = 0
for _h in range(8):
    BIAS_BASE.append(_b)
    _b += (512 // SUBW[_h]) * 64
assert _b == 768


class Prog:
    EPOCH = 8000

    def __init__(self):
        self.ops = []
        self.bars = []

    def add(self, eng, fn, reads=(), writes=(), dma=None):
        self.ops.append((eng, fn, tuple(reads), tuple(writes), dma))

    def pe(self, fn, r=(), w=()):
        self.add('pe', fn, r, w)

    def act(self, fn, r=(), w=()):
        self.add('act', fn, r, w)

    def dve(self, fn, r=(), w=()):
        self.add('dve', fn, r, w)

    def pool(self, fn, r=(), w=()):
        self.add('pool', fn, r, w)

    def dma(self, q, semkey, fn, r=(), w=()):
        self.add(q, fn, r, w, dma=semkey)

    def barrier(self):
        self.bars.append(len(self.ops))

    def analyze(self):
        ops = self.ops
        n = len(ops)
        last_writer = {}
        readers = {}
        deps = [set() for _ in range(n)]
        eng_pos = [0] * n
        cnt = {}
        bars = set(self.bars)
        last_by_eng = {}
        last_by_dma = {}
        pending = {}
        for i, (eng, fn, rd, wr, dma) in enumerate(ops):
            if i in bars:
                snap = set(last_by_eng.values()) | set(last_by_dma.values())
                for e in ('pe', 'act', 'dve', 'pool', 'sp'):
                    pending[e] = snap
            if pending.get(eng) is not None:
                deps[i] |= pending[eng]
                pending[eng] = None
            eng_pos[i] = cnt.get(eng, 0)
            cnt[eng] = eng_pos[i] + 1
            for t in rd:
                if t in last_writer:
                    deps[i].add(last_writer[t])
            for t in wr:
                if t in last_writer:
                    deps[i].add(last_writer[t])
                for x in readers.get(t, {}).values():
                    if x != i:
                        deps[i].add(x)
            for t in wr:
                last_writer[t] = i
                readers[t] = {}
            for t in rd:
                if t not in wr:
                    readers.setdefault(t, {})[eng if dma is None else ('dma', dma)] = i
            if dma is None:
                last_by_eng[eng] = i
            else:
                last_by_dma[dma] = i
        signal = [False] * n
        for i in range(n):
            keep = set()
            eng = ops[i][0]
            for p in deps[i]:
                peng, pdma = ops[p][0], ops[p][4]
                if pdma is None and peng == eng and eng == 'pe':
                    continue
                keep.add(p)
                signal[p] = True
            deps[i] = keep
        sigval = [None] * n
        ccount = {}
        dcount = {}
        for i in range(n):
            eng, fn, rd, wr, dma = ops[i]
            if dma is not None:
                dcount[dma] = dcount.get(dma, 0) + 1
                sigval[i] = (('dma', dma), dcount[dma] * 16)
            elif signal[i]:
                c = ccount.get(eng, 0)
                ccount[eng] = c + 1
                sigval[i] = ((eng, c // self.EPOCH), c % self.EPOCH + 1)
        waited = {}
        waits = [None] * n
        for i in range(n):
            eng = ops[i][0]
            need = {}
            for p in deps[i]:
                k, v = sigval[p]
                if need.get(k, 0) < v:
                    need[k] = v
            w = waited.setdefault(eng, {})
            lst = []
            for k, v in need.items():
                if w.get(k, 0) >= v:
                    continue
                w[k] = v
                lst.append((k, v))
            waits[i] = lst
        self.sigval = sigval
        self.waits = waits
        keys = set()
        for s in sigval:
            if s is not None:
                keys.add(s[0])
        self.semkeys = sorted(keys, key=str)
        return self

    def emit(self, nc, final_waits=()):
        self.analyze()
        ops = self.ops
        with contextlib.ExitStack() as st:
            sems = {}
            for k in self.semkeys:
                sems[k] = st.enter_context(nc.semaphore("s_" + "_".join(str(x) for x in k)))
            block = st.enter_context(nc.Block())
            per_eng = {}
            for i, o in enumerate(ops):
                per_eng.setdefault(o[0], []).append(i)
            totals = {}
            for s in self.sigval:
                if s is not None and s[0][0] == 'dma':
                    totals[s[0]] = max(totals.get(s[0], 0), s[1])

            def run(engobj, name):
                for i in per_eng.get(name, ()):
                    for k, v in self.waits[i]:
                        engobj.wait_ge(sems[k], v)
                    inst = ops[i][1](engobj)
                    s = self.sigval[i]
                    if s is not None:
                        inst.then_inc(sems[s[0]], 16 if s[0][0] == 'dma' else 1)
                if name == 'sp':
                    for key in final_waits:
                        k = ('dma', key)
                        engobj.wait_ge(sems[k], totals[k])

            @block.sync
            def _(e):
                run(e, 'sp')

            @block.tensor
            def _(e):
                run(e, 'pe')

            @block.scalar
            def _(e):
                run(e, 'act')

            @block.vector
            def _(e):
                run(e, 'dve')

            @block.gpsimd
            def _(e):
                run(e, 'pool')


def build(dbg=False):
    nc = bass.Bass("TRN2", target_bir_lowering=False)

    def din(name, shape, dt=F32):
        return nc.dram_tensor(name, shape, dt, kind="ExternalInput").ap()

    def dscr(name, shape, dt):
        if dbg and name in ("KT", "VS", "QT", "AT", "RN", "GA", "GB", "X1"):
            return nc.dram_tensor(name, shape, dt, kind="ExternalOutput").ap()
        return nc.dram_tensor(name, shape, dt).ap()

    xT_seq = din("xT_seq", [D, S])
    xT_own = din("xT_own", [D, NOWN])
    w_in = din("w_in", [D, 7168])
    w_r = din("w_r", [8, 128, 128])
    w_i = din("w_i", [8, 128, 128])
    w_att = din("w_att", [D, D])
    w_rnn = din("w_rnn", [D, D])
    w_o = din("w_o", [D, D])
    w_ff1 = din("w_ff1", [D, 4096])
    w_ff2 = din("w_ff2", [4096, D])
    pvec = din("pvec", [128, NPV])
    cst = din("cst", [128, NCST])
    amask = din("amask", [128, 8, 512], BF16)
    ident_in = din("ident_in", [128, 128], BF16)
    outT = nc.dram_tensor("outT", [D, NOWN], F32, kind="ExternalOutput").ap()

    WIN = dscr("WIN", [128, 8, 7168], BF16)
    WATT = dscr("WATT", [128, 8, D], BF16)
    WRNN = dscr("WRNN", [128, 8, D], BF16)
    WO = dscr("WO", [128, 8, D], BF16)
    W1 = dscr("W1", [128, 8, 4096], BF16)
    W2 = dscr("W2", [8, 128, 32, 128], BF16)
    WR = dscr("WR", [128, 8, 128], BF16)
    WI = dscr("WI", [128, 8, 128], BF16)
    KT = dscr("KT", [8, 128, S], BF16)
    VS = dscr("VS", [8, 128, 64, 128], BF16)
    QT = dscr("QT", [8, 128, NOWN], BF16)
    AT = dscr("AT", [8, 128, NOWN], BF16)
    RN = dscr("RN", [8, 128, NOWN], BF16)
    GA = dscr("GA", [8, 128, NOWN], BF16)
    GB = dscr("GB", [8, 128, NOWN], BF16)
    X1 = dscr("X1", [8, 128, NOWN], F32)

    P = Prog()
    NBA = 99000
    with contextlib.ExitStack() as st:
        def sb(name, shape, dt):
            return st.enter_context(nc.sbuf_tensor(name, shape, dt))

        pv = sb("pv", [128, NPV], F32)
        cs = sb("cs", [128, NCST], F32)
        cc = sb("cc", [128, 32], F32)
        lt = sb("lt", [128, 128], F32)
        ones = sb("ones", [128, 128], BF16)
        wr_sb = sb("wr_sb", [128, 8, 128], BF16)
        wi_sb = sb("wi_sb", [128, 8, 128], BF16)
        saved = sb("saved", [128, 8, 16, 4], F32)
        sel = sb("sel", [128, 8, 8, 4], F32)
        halo = sb("halo", [128, 8, 3], F32)
        state = sb("state", [128, 8], F32)
        arena = sb("arena", [128, NBA], BF16)
        ident = sb("ident", [128, 128], BF16)
        psall = st.enter_context(nc.psum_tensor("psall", [128, 8 * 512], F32))
        banks = [psall[:, i * 512:(i + 1) * 512] for i in range(8)]

        class Arena:
            def __init__(self):
                self.o = 0

            def B(self, n):
                a = arena[:, self.o:self.o + n]
                self.o += (n + 15) // 16 * 16
                assert self.o <= NBA, self.o
                return a

            def F(self, n):
                a = arena[:, self.o:self.o + 2 * n].bitcast(F32)
                self.o += (2 * n + 15) // 16 * 16
                assert self.o <= NBA, self.o
                return a

        def c3(ap, c):
            return ap.rearrange("p (c t) -> p c t", c=c)

        NJ = NOWN // T
        bank_ctr = [0]

        def next_bank():
            i = bank_ctr[0] % 8
            bank_ctr[0] += 1
            return i

        MC1 = cs[:, 0:1]
        MC0 = cs[:, 1:2]

        P.dma('pool', 'cv_r', lambda e: e.dma_start(out=WR, in_=w_r.rearrange("n c d -> c n d")), w=['WR'])
        P.dma('pool', 'cv_i', lambda e: e.dma_start(out=WI, in_=w_i.rearrange("n c d -> c n d")), w=['WI'])
        pool_bg = []

        def cv_win(c0, c1, key, ck, step):
            return ('cv_' + key, (lambda e: e.dma_start(
                out=WIN[:, ck:ck + step, c0:c1],
                in_=w_in[ck * 128:(ck + step) * 128, c0:c1].rearrange("(c p) n -> p c n", p=128))), 'WIN' + key)

        for ck in range(0, 8, 2):
            k_, fn_, tok_ = cv_win(3072, 4096, 'x', ck, 2)
            P.dma('pool', k_, fn_, w=[tok_])
        for (c0, c1, key) in ((1024, 2048, 'k'), (2048, 3072, 'v')):
            for ck in range(0, 8, 2):
                pool_bg.append(cv_win(c0, c1, key, ck, 2))
        for ck in range(0, 8, 2):
            pool_bg.append(cv_win(0, 1024, 'a', ck, 2))
            pool_bg.append(cv_win(4096, 7168, 'a', ck, 2))

        def emit_bg(n):
            for _ in range(n):
                if pool_bg:
                    k_, fn_, tok_ = pool_bg.pop(0)
                    P.dma('pool', k_, fn_, w=[tok_])

        P.dma('sp', 'ld_pv', lambda e: e.dma_start(out=pv[:], in_=pvec), w=['pv'])
        P.dma('sp', 'ld_cs', lambda e: e.dma_start(out=cs[:], in_=cst), w=['cs'])
        P.dma('sp', 'ld_id', lambda e: e.dma_start(out=ident[:], in_=ident_in), w=['ident'])
        P.dve(lambda e: e.memset(ones[:], 1.0), w=['ones'])
        P.dve(lambda e: e.memset(halo[:], 0.0), w=['halo'])
        P.dve(lambda e: e.memset(state[:], 0.0), w=['state'])
        P.act(lambda e: e.activation(out=cc[:, 20:28], in_=pv[:, 96:104], func=AF.Exp, scale=-1.0), r=['pv'], w=['cc_t'])
        P.act(lambda e: e.activation(out=cc[:, 20:28], in_=cc[:, 20:28], func=AF.Ln, bias=1.0, scale=1.0), r=['cc_t'], w=['cc_t'])
        P.dve(lambda e: e.tensor_scalar_mul(out=cc[:, 0:8], in0=cc[:, 20:28], scalar1=-8.0), r=['cc_t'], w=['cc_ca'])
        P.dve(lambda e: e.tensor_scalar_mul(out=cc[:, 8:16], in0=cc[:, 20:28], scalar1=-16.0), r=['cc_t'], w=['cc_ca'])
        P.dve(lambda e: e.tensor_tensor(out=lt[:, 0:64], in0=pv[:, 105:169], in1=pv[:, 169:233], op=ALU.mult), r=['pv'], w=['lt'])
        P.dve(lambda e: e.tensor_tensor(out=lt[:, 64:128], in0=pv[:, 233:297], in1=pv[:, 297:361], op=ALU.mult), r=['pv'], w=['lt'])
        P.dve(lambda e: e.reduce_sum(out=cc[:, 28:29], in_=lt[:, 0:64], axis=AX.X), r=['lt'], w=['cc_l'])
        P.dve(lambda e: e.reduce_sum(out=cc[:, 29:30], in_=lt[:, 64:128], axis=AX.X), r=['lt'], w=['cc_l'])
        P.act(lambda e: e.activation(out=cc[:, 30:32], in_=cc[:, 28:30], func=AF.Exp), r=['cc_l'], w=['cc_l2'])
        P.dve(lambda e: e.tensor_tensor(out=cc[:, 18:19], in0=cc[:, 30:31], in1=cc[:, 31:32], op=ALU.subtract), r=['cc_l2'], w=['cc_l3'])
        P.dve(lambda e: e.tensor_scalar(out=cc[:, 16:17], in0=cc[:, 18:19], scalar1=-1.0, scalar2=-0.2, op0=ALU.mult, op1=ALU.add),
              r=['cc_l3'], w=['cc_lam'])
        P.dve(lambda e: e.tensor_scalar_mul(out=cc[:, 17:18], in0=pv[:, 104:105], scalar1=0.8), r=['pv'], w=['cc_gs'])
        NEGLAM = cc[:, 16:17]
        GS = cc[:, 17:18]

        def late_conversions():
            for (src, dst, key) in ((w_att, WATT, 'WATT'), (w_rnn, WRNN, 'WRNN'), (w_o, WO, 'WO')):
                for ck in range(0, 8, 4):
                    P.dma('pool', 'cv_' + key, (lambda ck, src, dst: lambda e: e.dma_start(
                        out=dst[:, ck:ck + 4, :],
                        in_=src[ck * 128:(ck + 4) * 128, :].rearrange("(c p) n -> p c n", p=128)))(ck, src, dst), w=[key])
            for ck in range(8):
                P.dma('pool', 'cv_f', (lambda ck: lambda e: e.dma_start(
                    out=W1[:, ck, :], in_=w_ff1[ck * 128:(ck + 1) * 128, :]))(ck), w=['W1'])
            for dc in range(8):
                P.dma('pool', 'cv_f', (lambda dc: lambda e: e.dma_start(
                    out=W2[dc], in_=w_ff2[:, dc * 128:(dc + 1) * 128].rearrange("(f p) j -> p f j", p=128)))(dc), w=['W2'])

        def load_x(xt, src, t):
            P.dma('sp', 'ld_x', lambda e: e.dma_start(
                out=xt, in_=src.rearrange("(c p) t -> p c t", p=128)[:, :, t * T:(t + 1) * T]), w=['xt'])

        def stats_norm(xt, xsq, rs, rstd, hT, hb, gcol, xtok='xt'):
            for ck in range(8):
                P.act((lambda ck: lambda e: e.activation(out=xsq[:, ck, :], in_=xt[:, ck, :], func=AF.Square))(ck),
                      r=[xtok], w=['xsq%d' % ck])
            b = next_bank()
            for ck in range(8):
                P.pe((lambda ck: lambda e: e.matmul(banks[b][:], lhsT=ones[:], rhs=xsq[:, ck, :], start=(ck == 0), stop=(ck == 7)))(ck),
                     r=['xsq%d' % ck, 'ones'], w=['pb%d' % b])
            P.act(lambda e: e.activation(out=rs, in_=banks[b][:], func=AF.Ln, scale=1.0 / D, bias=EPS), r=['pb%d' % b], w=['rs'])
            P.act(lambda e: e.activation(out=rstd, in_=rs, func=AF.Exp, scale=-0.5), r=['rs'], w=['rstd'])
            for ck in range(8):
                P.dve((lambda ck: lambda e: e.scalar_tensor_tensor(
                    out=hT[:, ck, :], in0=xt[:, ck, :], scalar=pv[:, gcol + ck:gcol + ck + 1], in1=rstd,
                    op0=ALU.mult, op1=ALU.mult))(ck), r=[xtok, 'rstd', 'pv'], w=['hT%d_%d' % (hb, ck)])

        evac_ctr = [0]

        def evac_copy(out_ap, b, wtok, in_ap=None, act_only=False):
            evac_ctr[0] += 1
            if in_ap is None:
                in_ap = banks[b][:]
            if act_only or evac_ctr[0] % 2 == 0:
                P.act(lambda e: e.copy(out=out_ap, in_=in_ap), r=['pb%d' % b], w=[wtok])
            else:
                P.dve(lambda e: e.tensor_copy(out=out_ap, in_=in_ap), r=['pb%d' % b], w=[wtok])

        def proj(b, wA, col0, hT, hb, ncol=128, wtok='wA'):
            for ck in range(8):
                P.pe((lambda ck: lambda e: e.matmul(banks[b][:], lhsT=wA[:, ck, col0:col0 + ncol], rhs=hT[:, ck, :],
                                                    start=(ck == 0), stop=(ck == 7)))(ck),
                     r=[wtok, 'hT%d_%d' % (hb, ck)], w=['pb%d' % b])

        class RnnPipe:
            def __init__(self, A, own, nsets, wA, xbcol0, gbcol0=None, rn_sb=None, nxb=8, wtok='wA'):
                self.own, self.wA, self.xbcol0, self.gbcol0, self.rn_sb, self.wtok = own, wA, xbcol0, gbcol0, rn_sb, wtok
                self.xbr = [A.F(520)[:, 0:515] for _ in range(nxb)]
                self.xctr = 0
                self.sets = []
                for i in range(nsets):
                    d = {k: A.F(512) for k in ('xc', 'rr', 'ii', 'aa')}
                    d['xcb'] = A.B(512)
                    if own:
                        d['gg'] = A.F(512)
                    d['id'] = i
                    self.sets.append(d)
                self.ctr = 0
                self.jobs = []
                self.after_first = None

            def add_tile(self, hT, hb, t=None, J=None, hook=None, post=(None, None)):
                for half in range(2):
                    ss = [self.sets[(self.ctr + i) % len(self.sets)] for i in range(4)]
                    self.ctr += 4
                    xbs = []
                    for i in range(4):
                        xi = self.xctr % len(self.xbr)
                        self.xctr += 1
                        xbs.append((self.xbr[xi], 'xb%d' % xi))
                    self.jobs.append(dict(hT=hT, hb=hb, t=t, J=J, ns=list(range(half * 4, half * 4 + 4)), ss=ss, xbs=xbs,
                                          hook=hook if half == 0 else None, post=post[half]))

            def run(self):
                nj = len(self.jobs)
                for k in range(nj + 2):
                    if 0 <= k - 2 < nj:
                        self.Z(self.jobs[k - 2])
                    if k < nj:
                        self.X(self.jobs[k])
                        if k == 0 and self.after_first is not None:
                            self.after_first()
                        else:
                            emit_bg(1)
                    if 0 <= k - 1 < nj:
                        self.Y(self.jobs[k - 1])
                    if k < nj:
                        self.X2(self.jobs[k])
                        if self.jobs[k]['post'] is not None:
                            self.jobs[k]['post']()

            def X(self, jb):
                own, t, J, hT, hb = self.own, jb['t'], jb['J'], jb['hT'], jb['hb']
                cw = lambda j, n: pv[:, 40 + j * 8 + n:41 + j * 8 + n]
                for n, (xb, xtok) in zip(jb['ns'], jb['xbs']):
                    if own:
                        P.pool((lambda xb, n: lambda e: e.tensor_copy(out=xb[:, 0:3], in_=sel[:, n, J, 0:3]))(xb, n), r=['sel'], w=[xtok])
                    else:
                        P.pool((lambda xb, n: lambda e: e.tensor_copy(out=xb[:, 0:3], in_=halo[:, n, :]))(xb, n), r=['halo%d' % n, 'halo'], w=[xtok])
                    b = next_bank()
                    proj(b, self.wA, self.xbcol0 + n * 128, hT, hb, wtok=self.wtok)
                    P.act((lambda xb, b: lambda e: e.copy(out=xb[:, 3:515], in_=banks[b][:]))(xb, b), r=['pb%d' % b], w=[xtok])
                    if not own:
                        P.pool((lambda xb, n: lambda e: e.tensor_copy(out=halo[:, n, :], in_=xb[:, 512:515]))(xb, n), r=[xtok], w=['halo%d' % n])
                        P.pool((lambda xb, n: lambda e: e.tensor_copy(out=saved[:, n, t, 0:3], in_=xb[:, 512:515]))(xb, n), r=[xtok], w=['saved'])
                for n, d, (xb, xtok) in zip(jb['ns'], jb['ss'], jb['xbs']):
                    xc, ct, i = d['xc'], d['aa'], d['id']
                    P.pool((lambda xb, xc, n: lambda e: e.tensor_scalar(out=xc, in0=xb[:, 3:515], scalar1=cw(3, n), scalar2=pv[:, 72 + n:73 + n],
                                                                        op0=ALU.mult, op1=ALU.add))(xb, xc, n), r=[xtok, 'pv'], w=['xc_%d' % i])
                    P.pool((lambda xb, ct, n: lambda e: e.tensor_scalar(out=ct, in0=xb[:, 2:514], scalar1=cw(2, n), scalar2=0.0,
                                                                        op0=ALU.mult, op1=ALU.add))(xb, ct, n), r=[xtok, 'pv'], w=['aa_%d' % i])
                    P.pool((lambda xc, ct: lambda e: e.tensor_tensor(out=xc, in0=xc, in1=ct, op=ALU.add))(xc, ct), r=['aa_%d' % i, 'xc_%d' % i], w=['xc_%d' % i])
                for n, d, (xb, xtok) in zip(jb['ns'], jb['ss'], jb['xbs']):
                    xc, i = d['xc'], d['id']
                    for j in (1, 0):
                        P.dve((lambda xb, xc, j, n: lambda e: e.scalar_tensor_tensor(out=xc, in0=xb[:, j:j + 512], scalar=cw(j, n), in1=xc,
                                                                                     op0=ALU.mult, op1=ALU.add))(xb, xc, j, n),
                              r=[xtok, 'pv', 'xc_%d' % i], w=['xc_%d' % i])

            def X2(self, jb):
                for n, d in zip(jb['ns'], jb['ss']):
                    P.act((lambda xc, xcb: lambda e: e.copy(out=xcb, in_=xc))(d['xc'], d['xcb']), r=['xc_%d' % d['id']], w=['xcb_%d' % d['id']])

            def Y(self, jb):
                own, hT, hb = self.own, jb['hT'], jb['hb']
                for n, d in zip(jb['ns'], jb['ss']):
                    xcb, i = d['xcb'], d['id']
                    br = next_bank()
                    P.pe((lambda br, n, xcb: lambda e: e.matmul(banks[br][:], lhsT=wr_sb[:, n, :], rhs=xcb, start=True, stop=True))(br, n, xcb),
                         r=['xcb_%d' % i, 'wr_sb'], w=['pb%d' % br])
                    P.act((lambda br, n, rr: lambda e: e.activation(out=rr, in_=banks[br][:], func=AF.Sigmoid, bias=pv[:, 80 + n:81 + n]))(br, n, d['rr']),
                          r=['pb%d' % br, 'pv'], w=['rr_%d' % i])
                    bi = next_bank()
                    P.pe((lambda bi, n, xcb: lambda e: e.matmul(banks[bi][:], lhsT=wi_sb[:, n, :], rhs=xcb, start=True, stop=True))(bi, n, xcb),
                         r=['xcb_%d' % i, 'wi_sb'], w=['pb%d' % bi])
                    P.act((lambda bi, n, ii: lambda e: e.activation(out=ii, in_=banks[bi][:], func=AF.Sigmoid, bias=pv[:, 88 + n:89 + n]))(bi, n, d['ii']),
                          r=['pb%d' % bi, 'pv'], w=['ii_%d' % i])
                for n, d in zip(jb['ns'], jb['ss']):
                    rr, aa, i = d['rr'], d['aa'], d['id']
                    P.act((lambda rr, aa, n: lambda e: e.activation(out=aa, in_=rr, func=AF.Exp, scale=cc[:, n:n + 1]))(rr, aa, n),
                          r=['rr_%d' % i, 'cc_ca'], w=['aa_%d' % i])
                    P.act((lambda rr, n: lambda e: e.activation(out=rr, in_=rr, func=AF.Exp, scale=cc[:, 8 + n:9 + n]))(rr, n),
                          r=['rr_%d' % i, 'cc_ca'], w=['rr_%d' % i])
                if jb['hook'] is not None:
                    jb['hook']()
                for n, d in zip(jb['ns'], jb['ss']):
                    rr, i = d['rr'], d['id']
                    P.act((lambda rr: lambda e: e.activation(out=rr, in_=rr, func=AF.Sqrt, scale=-1.0, bias=1.0))(rr), r=['rr_%d' % i], w=['rr_%d' % i])
                if own:
                    for n, d in zip(jb['ns'], jb['ss']):
                        bg = next_bank()
                        proj(bg, self.wA, self.gbcol0 + n * 128, hT, hb, wtok=self.wtok)
                        P.act((lambda gg, bg: lambda e: e.activation(out=gg, in_=banks[bg][:], func=AF.Gelu_apprx_tanh))(d['gg'], bg),
                              r=['pb%d' % bg], w=['gg_%d' % d['id']])

            def Z(self, jb):
                own, t, J = self.own, jb['t'], jb['J']
                for n, d in zip(jb['ns'], jb['ss']):
                    xc, rr, ii, aa, i = d['xc'], d['rr'], d['ii'], d['aa'], d['id']
                    if not own and t == 0:
                        P.dve((lambda rr: lambda e: e.memset(rr[:, 0:1], 1.0))(rr), r=['rr_%d' % i], w=['rr_%d' % i])
                    elif own and J == 0:
                        P.dve((lambda rr: lambda e: e.tensor_scalar(out=rr[:, 0:1], in0=rr[:, 0:1], scalar1=MC1, scalar2=MC0,
                                                                    op0=ALU.mult, op1=ALU.add))(rr), r=['rr_%d' % i, 'cs'], w=['rr_%d' % i])
                    P.dve((lambda ii, xc: lambda e: e.tensor_tensor(out=ii, in0=ii, in1=xc, op=ALU.mult))(ii, xc), r=['ii_%d' % i, 'xc_%d' % i], w=['ii_%d' % i])
                    P.dve((lambda ii, rr: lambda e: e.tensor_tensor(out=ii, in0=ii, in1=rr, op=ALU.mult))(ii, rr), r=['ii_%d' % i, 'rr_%d' % i], w=['ii_%d' % i])
                    if own:
                        init_ap, init_tok = sel[:, n, J, 3:4], ['sel']
                    else:
                        init_ap, init_tok = state[:, n:n + 1], ['state%d' % n, 'state']
                    P.dve((lambda xc, aa, ii, init_ap: lambda e: e.tensor_tensor_scan(out=xc, data0=aa, data1=ii, initial=init_ap,
                                                                                      op0=ALU.mult, op1=ALU.add))(xc, aa, ii, init_ap),
                          r=['aa_%d' % i, 'ii_%d' % i] + init_tok, w=['xc_%d' % i])
                    if own:
                        P.dve((lambda xc, gg, n: lambda e: e.tensor_tensor(out=self.rn_sb[:, n, :], in0=xc, in1=gg, op=ALU.mult))(xc, d['gg'], n),
                              r=['xc_%d' % i, 'gg_%d' % i], w=['rn_sb%d' % n])
                    else:
                        P.dve((lambda xc, n: lambda e: e.tensor_copy(out=state[:, n:n + 1], in_=xc[:, 511:512]))(xc, n), r=['xc_%d' % i, 'state'], w=['state%d' % n])
                        P.dve((lambda xc, n: lambda e: e.tensor_copy(out=saved[:, n, t, 3:4], in_=xc[:, 511:512]))(xc, n), r=['xc_%d' % i], w=['saved_s'])
                if own and jb['ns'][0] == 4:
                    P.dma('sp', 'st_rn', (lambda J: lambda e: e.dma_start(
                        out=RN[:, :, J * T:(J + 1) * T].rearrange("h p t -> p h t"), in_=self.rn_sb))(J), r=['rn_sb%d' % q for q in range(8)])

        NT = S // T

        def phase1():
            A = Arena()
            wA = c3(A.B(8 * 3072), 8)
            xsq = c3(A.B(8 * T), 8)
            hTs = [c3(A.B(8 * T), 8) for _ in range(2)]
            kt_sb = c3(A.B(8 * T), 8)
            v_sb = A.B(8 * 4 * 128).rearrange("p (h b e) -> p h b e", h=8, b=4)
            xt = c3(A.F(8 * T), 8)
            rs = A.F(T)
            rstd = A.F(T)
            pipe = RnnPipe(A, False, 8, wA, 2048, nxb=6, wtok='wAx')
            load_x(xt, xT_seq, 0)
            P.dma('sp', 'ld_wr', lambda e: e.dma_start(out=wr_sb[:], in_=WR), r=['WR'], w=['wr_sb'])
            P.dma('sp', 'ld_wi', lambda e: e.dma_start(out=wi_sb[:], in_=WI), r=['WI'], w=['wi_sb'])

            def load_w(c0, key):
                for ck in range(0, 8, 4):
                    P.dma('sp', 'ld_wA' + key, (lambda ck: lambda e: e.dma_start(out=wA[:, ck:ck + 4, c0 - 1024:c0], in_=WIN[:, ck:ck + 4, c0:c0 + 1024]))(ck),
                          r=['WIN' + key], w=['wA' + key])

            load_w(3072, 'x')

            def after_first():
                emit_bg(8)
                load_w(1024, 'k')
                load_w(2048, 'v')
            pipe.after_first = after_first

            def k_part(t, hT, hb):
                for h in range(8):
                    b = next_bank()
                    proj(b, wA, h * 128, hT, hb, wtok='wAk')
                    evac_copy(kt_sb[:, h, :], b, 'kt_sb%d' % h, act_only=True)
                P.dma('sp', 'st_kt', lambda e: e.dma_start(
                    out=KT[:, :, t * T:(t + 1) * T].rearrange("h p t -> p h t"), in_=kt_sb), r=['kt_sb%d' % i for i in range(8)])

            def v_part(t, hT, hb):
                for j in range(4):
                    for half in range(2):
                        b = next_bank()
                        for ck in range(8):
                            P.pe((lambda ck, j, half, b: lambda e: e.matmul(
                                banks[b][:], lhsT=hT[:, ck, j * 128:(j + 1) * 128], rhs=wA[:, ck, 1024 + half * 512:1536 + half * 512],
                                start=(ck == 0), stop=(ck == 7)))(ck, j, half, b), r=['wAv', 'hT%d_%d' % (hb, ck)], w=['pb%d' % b])
                        evac_copy(v_sb[:, half * 4:(half + 1) * 4, j, :], b, 'v_sb%d' % (j * 2 + half),
                                  in_ap=banks[b][:].rearrange("p (h e) -> p h e", h=4))
                P.dma('sp', 'st_v', lambda e: e.dma_start(
                    out=VS[:, :, 4 * t:4 * t + 4, :].rearrange("h p b e -> p h b e"), in_=v_sb), r=['v_sb%d' % i for i in range(8)])

            stats_norm(xt, xsq, rs, rstd, hTs[0], 0, 0)
            for t in range(NT):
                hb = t % 2
                hook = None
                if t + 1 < NT:
                    hook = (lambda hb, t: lambda: (load_x(xt, xT_seq, t + 1), stats_norm(xt, xsq, rs, rstd, hTs[1 - hb], 1 - hb, 0)))(hb, t)
                post = ((lambda t, hb: lambda: k_part(t, hTs[hb], hb))(t, hb), (lambda t, hb: lambda: v_part(t, hTs[hb], hb))(t, hb))
                pipe.add_tile(hTs[hb], hb, t=t, hook=hook, post=post)
            pipe.run()
            emit_bg(100)
        phase1()

        P.barrier()

        def phase2():
            A = Arena()
            wA = c3(A.B(8 * 3072), 8)
            xsq = c3(A.B(8 * T), 8)
            hTs = [c3(A.B(8 * T), 8) for _ in range(2)]
            qt_sb = c3(A.B(8 * T), 8)
            g_sb = c3(A.B(16 * T), 16)
            xt = c3(A.F(8 * T), 8)
            rs = A.F(T)
            rstd = A.F(T)
            for ck in range(8):
                P.dma('sp', 'ld_wA', (lambda ck: lambda e: e.dma_start(out=wA[:, ck, 0:1024], in_=WIN[:, ck, 0:1024]))(ck), r=['WINa'], w=['wA'])
                P.dma('sp', 'ld_wA', (lambda ck: lambda e: e.dma_start(out=wA[:, ck, 1024:3072], in_=WIN[:, ck, 5120:7168]))(ck), r=['WINa'], w=['wA'])
            load_x(xt, xT_own, 0)
            stats_norm(xt, xsq, rs, rstd, hTs[0], 0, 0)
            for J in range(NJ):
                hb = J % 2
                hT = hTs[hb]
                if J + 1 < NJ:
                    load_x(xt, xT_own, J + 1)
                for h in range(8):
                    b = next_bank()
                    proj(b, wA, h * 128, hT, hb)
                    evac_copy(qt_sb[:, h, :], b, 'qt_sb%d' % h)
                P.dma('sp', 'st_q', (lambda J: lambda e: e.dma_start(
                    out=QT[:, :, J * T:(J + 1) * T].rearrange("h p t -> p h t"), in_=qt_sb))(J), r=['qt_sb%d' % i for i in range(8)])
                if J + 1 < NJ:
                    stats_norm(xt, xsq, rs, rstd, hTs[1 - hb], 1 - hb, 0)
                for gch in range(16):
                    b = next_bank()
                    proj(b, wA, 1024 + gch * 128, hT, hb)
                    P.act((lambda gch, b: lambda e: e.activation(out=g_sb[:, gch, :], in_=banks[b][:], func=AF.Sigmoid,
                                                                 bias=pv[:, 24 + gch:25 + gch]))(gch, b), r=['pb%d' % b, 'pv'], w=['g_sb%d' % gch])
                P.dma('sp', 'st_ga', (lambda J: lambda e: e.dma_start(
                    out=GA[:, :, J * T:(J + 1) * T].rearrange("h p t -> p h t"), in_=g_sb[:, 0:8, :]))(J), r=['g_sb%d' % i for i in range(8)])
                P.dma('sp', 'st_gb', (lambda J: lambda e: e.dma_start(
                    out=GB[:, :, J * T:(J + 1) * T].rearrange("h p t -> p h t"), in_=g_sb[:, 8:16, :]))(J), r=['g_sb%d' % i for i in range(8, 16)])
        phase2()

        P.barrier()

        def phase3():
            A = Arena()
            wA = c3(A.B(8 * 2048), 8)
            xsq = c3(A.B(8 * T), 8)
            hTs = [c3(A.B(8 * T), 8) for _ in range(2)]
            rn_sb = c3(A.B(8 * T), 8)
            xt = c3(A.F(8 * T), 8)
            rs = A.F(T)
            rstd = A.F(T)
            pipe = RnnPipe(A, True, 8, wA, 0, gbcol0=1024, rn_sb=rn_sb)
            for ck in range(8):
                P.dma('sp', 'ld_wA', (lambda ck: lambda e: e.dma_start(out=wA[:, ck, :], in_=WIN[:, ck, 3072:5120]))(ck), r=['WINx', 'WINa'], w=['wA'])
            for J in range(NJ):
                P.dve((lambda J: lambda e: e.tensor_scalar_mul(out=sel[:, :, J, :], in0=saved[:, :, 2 * J, :], scalar1=MC1))(J),
                      r=['saved', 'saved_s', 'cs'], w=['sel'])
                if J >= 1:
                    P.dve((lambda J: lambda e: e.scalar_tensor_tensor(out=sel[:, :, J, :], in0=saved[:, :, 2 * J - 1, :], scalar=MC0,
                                                                      in1=sel[:, :, J, :], op0=ALU.mult, op1=ALU.add))(J),
                          r=['saved', 'saved_s', 'cs', 'sel'], w=['sel'])
            load_x(xt, xT_own, 0)
            stats_norm(xt, xsq, rs, rstd, hTs[0], 0, 0)
            for J in range(NJ):
                hb = J % 2
                hook = None
                if J + 1 < NJ:
                    hook = (lambda hb, J: lambda: (load_x(xt, xT_own, J + 1), stats_norm(xt, xsq, rs, rstd, hTs[1 - hb], 1 - hb, 0)))(hb, J)
                pipe.add_tile(hTs[hb], hb, J=J, hook=hook)
            pipe.run()
        phase3()

        def phase4():
            P.barrier()
            A = Arena()
            KTh = [A.B(S) for _ in range(2)]
            Vh = [A.B(S).rearrange("p (b e) -> p b e", e=128) for _ in range(2)]
            Qh = [A.B(NOWN) for _ in range(2)]
            am_sb = c3(A.B(8 * 512), 8)
            NPB = 3
            Pb2 = [A.B(1024) for _ in range(NPB)]
            Pb = [[Pb2[i][:, m * 512:(m + 1) * 512] for i in range(NPB)] for m in range(2)]
            osq = A.B(512)
            att_sb = [A.B(512) for _ in range(2)]
            r1 = A.F(512)
            r2 = A.F(512)
            t1 = A.F(512)
            t2 = A.F(512)
            oos = [A.F(512) for _ in range(2)]
            rsa = A.F(512)
            rstda = A.F(512)
            P.dma('sp', 'ld_c2', lambda e: e.dma_start(out=am_sb, in_=amask), w=['am_sb'])
            late_conversions()
            SB = [0, 1, 2, 3]
            BO1, BO2, BL1, BL2 = 4, 5, 6, 7
            s_ctr = 0
            p_ctr = 0
            pending_epi = [None]

            def load_head(h):
                hbuf = h % 2
                P.dma('sp', 'ld_kt%d' % hbuf, lambda e: e.dma_start(out=KTh[hbuf], in_=KT[h]), w=['KTh%d' % hbuf])
                P.dma('sp', 'ld_v%d' % hbuf, lambda e: e.dma_start(out=Vh[hbuf], in_=VS[h]), w=['Vh%d' % hbuf])
                P.dma('sp', 'ld_q%d' % hbuf, lambda e: e.dma_start(out=Qh[hbuf], in_=QT[h]), w=['Qh%d' % hbuf])

            def emit_qk(h, J, kb, slot):
                hbuf = h % 2
                delta = kb - 8 * J
                for m in range(2):
                    b = 2 * slot + m
                    P.pe((lambda m, b: lambda e: e.matmul(
                        banks[b][:], lhsT=KTh[hbuf][64 * m:64 * m + 64, kb * 128:(kb + 1) * 128],
                        rhs=Qh[hbuf][64 * m:64 * m + 64, J * T:(J + 1) * T], start=True, stop=(delta < 0)))(m, b),
                        r=['KTh%d' % hbuf, 'Qh%d' % hbuf], w=['pb%d' % b])
                if delta >= 0:
                    for m in range(2):
                        b = 2 * slot + m
                        P.pe((lambda b: lambda e: e.matmul(banks[b][:], lhsT=ident[:], rhs=am_sb[:, delta, :], start=False, stop=True))(b),
                             r=['ident', 'am_sb'], w=['pb%d' % b])

            def emit_exp(h, J, kb, slot, pi):
                w = SUBW[h]
                delta = kb - 8 * J
                for m in range(2):
                    b = 2 * slot + m
                    for s in range(512 // w):
                        col = 2 + BIAS_BASE[h] + s * 64 + (delta + 56)
                        P.act((lambda m, b, s, col: lambda e: e.activation(
                            out=Pb[m][pi][:, s * w:(s + 1) * w], in_=banks[b][:, s * w:(s + 1) * w], func=AF.Exp,
                            bias=cs[:, col:col + 1], scale=0.125))(m, b, s, col), r=['pb%d' % b, 'cs'], w=['P%d_%d' % (m, pi)])

            def emit_pv(h, J, kb, first, last, pi):
                hbuf = h % 2
                for m in range(2):
                    bo = (BO1, BO2)[m]
                    bl = (BL1, BL2)[m]
                    P.pe((lambda m, bo: lambda e: e.matmul(banks[bo][:], lhsT=Vh[hbuf][:, kb, :], rhs=Pb[m][pi],
                                                           start=first, stop=last))(m, bo),
                         r=['Vh%d' % hbuf, 'P%d_%d' % (m, pi)], w=['pb%d' % bo])
                    P.pe((lambda m, bl: lambda e: e.matmul(banks[bl][:], lhsT=ones[:], rhs=Pb[m][pi],
                                                           start=first, stop=last))(m, bl),
                         r=['ones', 'P%d_%d' % (m, pi)], w=['pb%d' % bl])

            epi_ctr = [0]

            def epi_part1(h, J):
                ei = epi_ctr[0] % 2
                epi_ctr[0] += 1
                oo = oos[ei]
                P.dve(lambda e: e.tensor_copy(out=t1, in_=banks[BO1][:]), r=['pb%d' % BO1], w=['t1'])
                P.dve(lambda e: e.tensor_copy(out=r1, in_=banks[BL1][:]), r=['pb%d' % BL1], w=['r1'])
                P.dve(lambda e: e.tensor_copy(out=t2, in_=banks[BO2][:]), r=['pb%d' % BO2], w=['t2'])
                P.dve(lambda e: e.tensor_copy(out=r2, in_=banks[BL2][:]), r=['pb%d' % BL2], w=['r2'])
                P.dve(lambda e: e.reciprocal(out=r1, in_=r1), r=['r1'], w=['r1'])
                P.dve(lambda e: e.reciprocal(out=r2, in_=r2), r=['r2'], w=['r2'])
                P.dve(lambda e: e.tensor_tensor(out=t1, in0=t1, in1=r1, op=ALU.mult), r=['t1', 'r1'], w=['t1'])
                P.dve(lambda e: e.tensor_tensor(out=t2, in0=t2, in1=r2, op=ALU.mult), r=['t2', 'r2'], w=['t2'])
                P.dve(lambda e: e.scalar_tensor_tensor(out=oo, in0=t2, scalar=NEGLAM, in1=t1, op0=ALU.mult, op1=ALU.add),
                      r=['t1', 't2', 'cc_lam'], w=['oo%d' % ei])
                P.dve(lambda e: e.tensor_tensor(out=osq, in0=oo, in1=oo, op=ALU.mult), r=['oo%d' % ei], w=['osq'])
                return (h, J, ei)

            def epi_part2(args, slot):
                h, J, ei = args
                oo = oos[ei]
                b = 2 * slot
                asb = att_sb[ei]
                P.pe(lambda e: e.matmul(banks[b][:], lhsT=ones[:], rhs=osq, start=True, stop=True), r=['osq', 'ones'], w=['pb%d' % b])
                P.act(lambda e: e.activation(out=rsa, in_=banks[b][:], func=AF.Ln, scale=1.0 / 128, bias=EPS), r=['pb%d' % b], w=['rsa'])
                P.act(lambda e: e.activation(out=rstda, in_=rsa, func=AF.Exp, scale=-0.5), r=['rsa'], w=['rstda'])
                P.dve(lambda e: e.scalar_tensor_tensor(out=asb, in0=oo, scalar=GS, in1=rstda, op0=ALU.mult, op1=ALU.mult),
                      r=['oo%d' % ei, 'rstda', 'cc_gs'], w=['att%d' % ei])
                P.dma('sp', 'st_at%d' % ei, lambda e: e.dma_start(out=AT[h][:, J * T:(J + 1) * T], in_=asb), r=['att%d' % ei])

            tiles = []
            for h in range(8):
                for J in range(NJ):
                    kbs = list(range(max(0, 8 * J - WINB[h]), 8 * J + 8))
                    for i, kb in enumerate(kbs):
                        tiles.append((h, J, kb, i, len(kbs)))
            NTI = len(tiles)
            load_head(0)
            load_head(1)
            for g in (0, 1):
                emit_qk(tiles[g][0], tiles[g][1], tiles[g][2], g % 2)
            last_slot = 0
            for g, (h, J, kb, i, n) in enumerate(tiles):
                if i == 0 and J == 0 and 1 <= h < 7:
                    load_head(h + 1)
                pi = p_ctr % NPB
                p_ctr += 1
                emit_exp(h, J, kb, g % 2, pi)
                if i == min(10, n - 1) and pending_epi[0] is not None:
                    epi_part2(pending_epi[0], g % 2)
                    pending_epi[0] = None
                if g + 2 < NTI:
                    h2, J2, kb2, _, _ = tiles[g + 2]
                    emit_qk(h2, J2, kb2, g % 2)
                emit_pv(h, J, kb, i == 0, i == n - 1, pi)
                if i == n - 1:
                    pending_epi[0] = epi_part1(h, J)
                last_slot = g % 2
            epi_part2(pending_epi[0], last_slot)

        phase4()

        def phase5():
            P.barrier()
            A = Arena()
            watt = c3(A.B(8 * D), 8)
            wrnn = c3(A.B(8 * D), 8)
            wo = c3(A.B(8 * D), 8)
            at_sb = [c3(A.B(8 * T), 8) for _ in range(2)]
            rn_in = [c3(A.B(8 * T), 8) for _ in range(2)]
            ga_sb = [c3(A.B(8 * T), 8) for _ in range(2)]
            gb_sb = [c3(A.B(8 * T), 8) for _ in range(2)]
            mT = c3(A.B(8 * T), 8)
            xts = [c3(A.F(8 * T), 8) for _ in range(2)]
            m1 = [A.F(T) for _ in range(2)]
            m2 = [A.F(T) for _ in range(2)]
            P.dma('sp', 'ld_watt', lambda e: e.dma_start(out=watt, in_=WATT), r=['WATT'], w=['watt'])
            P.dma('sp', 'ld_wrnn', lambda e: e.dma_start(out=wrnn, in_=WRNN), r=['WRNN'], w=['wrnn'])

            def loads(J):
                p = J % 2
                sl = slice(J * T, (J + 1) * T)
                P.dma('sp', 'ld_at%d' % p, lambda e: e.dma_start(out=at_sb[p], in_=AT[:, :, sl].rearrange("h p t -> p h t")), w=['at_sb%d' % p])
                P.dma('sp', 'ld_rn%d' % p, lambda e: e.dma_start(out=rn_in[p], in_=RN[:, :, sl].rearrange("h p t -> p h t")), w=['rn_in%d' % p])
                P.dma('sp', 'ld_ga%d' % p, lambda e: e.dma_start(out=ga_sb[p], in_=GA[:, :, sl].rearrange("h p t -> p h t")), w=['ga_sb%d' % p])
                P.dma('sp', 'ld_gb%d' % p, lambda e: e.dma_start(out=gb_sb[p], in_=GB[:, :, sl].rearrange("h p t -> p h t")), w=['gb_sb%d' % p])
                P.dma('sp', 'ld_x5_%d' % p, lambda e: e.dma_start(
                    out=xts[p], in_=xT_own.rearrange("(c p) t -> p c t", p=128)[:, :, sl]), w=['xt5_%d' % p])

            loads(0)
            P.dma('sp', 'ld_wo', lambda e: e.dma_start(out=wo, in_=WO), r=['WO'], w=['wo'])
            for J in range(NJ):
                p = J % 2
                sl = slice(J * T, (J + 1) * T)
                xt = xts[p]
                if J + 1 < NJ:
                    loads(J + 1)
                for dc in range(8):
                    ba = next_bank()
                    for ck in range(8):
                        P.pe((lambda ck, dc, ba, p: lambda e: e.matmul(banks[ba][:], lhsT=watt[:, ck, dc * 128:(dc + 1) * 128], rhs=at_sb[p][:, ck, :],
                                                                       start=(ck == 0), stop=(ck == 7)))(ck, dc, ba, p), r=['watt', 'at_sb%d' % p], w=['pb%d' % ba])
                    bb = next_bank()
                    for ck in range(8):
                        P.pe((lambda ck, dc, bb, p: lambda e: e.matmul(banks[bb][:], lhsT=wrnn[:, ck, dc * 128:(dc + 1) * 128], rhs=rn_in[p][:, ck, :],
                                                                       start=(ck == 0), stop=(ck == 7)))(ck, dc, bb, p), r=['wrnn', 'rn_in%d' % p], w=['pb%d' % bb])
                    mi = dc % 2
                    P.dve((lambda dc, ba, mi, p: lambda e: e.tensor_tensor(out=m1[mi], in0=banks[ba][:], in1=ga_sb[p][:, dc, :], op=ALU.mult))(dc, ba, mi, p),
                          r=['pb%d' % ba, 'ga_sb%d' % p], w=['m1_%d' % mi])
                    P.dve((lambda dc, bb, mi, p: lambda e: e.tensor_tensor(out=m2[mi], in0=banks[bb][:], in1=gb_sb[p][:, dc, :], op=ALU.mult))(dc, bb, mi, p),
                          r=['pb%d' % bb, 'gb_sb%d' % p], w=['m2_%d' % mi])
                    P.pool((lambda dc, mi: lambda e: e.tensor_tensor(out=mT[:, dc, :], in0=m1[mi], in1=m2[mi], op=ALU.add))(dc, mi),
                           r=['m1_%d' % mi, 'm2_%d' % mi], w=['mT%d' % dc])
                for dc in range(8):
                    b = next_bank()
                    for ck in range(8):
                        P.pe((lambda ck, dc, b: lambda e: e.matmul(banks[b][:], lhsT=wo[:, ck, dc * 128:(dc + 1) * 128], rhs=mT[:, ck, :],
                                                                   start=(ck == 0), stop=(ck == 7)))(ck, dc, b), r=['wo', 'mT%d' % ck], w=['pb%d' % b])
                    P.dve((lambda dc, b, xt: lambda e: e.tensor_tensor(out=xt[:, dc, :], in0=xt[:, dc, :], in1=banks[b][:], op=ALU.add))(dc, b, xt),
                          r=['pb%d' % b, 'xt5_%d' % p], w=['x1_%d_%d' % (p, dc)])
                P.dma('sp', 'st_x1_%d' % p, (lambda sl, xt: lambda e: e.dma_start(out=X1[:, :, sl].rearrange("h p t -> p h t"), in_=xt))(sl, xt),
                      r=['x1_%d_%d' % (p, dc) for dc in range(8)] + ['xt5_%d' % p], w=['xt5_%d' % p])

        phase5()

        def phase6():
            P.barrier()
            A = Arena()
            w1 = c3(A.B(8 * 4096), 8)
            w2b = [A.B(32 * 128).rearrange("p (f j) -> p f j", j=128) for _ in range(2)]
            xsq = c3(A.B(8 * T), 8)
            hTs = [c3(A.B(8 * T), 8) for _ in range(2)]
            aT = c3(A.B(32 * T), 32)
            xts = [c3(A.F(8 * T), 8) for _ in range(2)]
            rs = A.F(T)
            rstd = A.F(T)
            rl = [A.F(T) for _ in range(2)]

            def load6(J):
                p = J % 2
                sl = slice(J * T, (J + 1) * T)
                P.dma('sp', 'ld_x6_%d' % p, lambda e: e.dma_start(out=xts[p], in_=X1[:, :, sl].rearrange("h p t -> p h t")), w=['xt6_%d' % p])

            load6(0)
            for ck in range(8):
                P.dma('sp', 'ld_w1', (lambda ck: lambda e: e.dma_start(out=w1[:, ck, :], in_=W1[:, ck, :]))(ck), r=['W1'], w=['w1'])
            stats_norm(xts[0], xsq, rs, rstd, hTs[0], 0, 8, xtok='xt6_0')
            w2_ctr = 0
            for J in range(NJ):
                p = J % 2
                sl = slice(J * T, (J + 1) * T)
                xt, hT = xts[p], hTs[p]
                if J + 1 < NJ:
                    load6(J + 1)
                for fc in range(32):
                    b = next_bank()
                    for ck in range(8):
                        P.pe((lambda ck, fc, b, hT: lambda e: e.matmul(banks[b][:], lhsT=w1[:, ck, fc * 128:(fc + 1) * 128], rhs=hT[:, ck, :],
                                                                       start=(ck == 0), stop=(ck == 7)))(ck, fc, b, hT), r=['w1', 'hT%d_%d' % (p, ck)], w=['pb%d' % b])
                    ri = fc % 2
                    P.act((lambda b, ri: lambda e: e.activation(out=rl[ri], in_=banks[b][:], func=AF.Relu))(b, ri), r=['pb%d' % b], w=['rl%d' % ri])
                    if fc % 2 == 0:
                        P.pool((lambda fc, ri: lambda e: e.tensor_tensor(out=aT[:, fc, :], in0=rl[ri], in1=rl[ri], op=ALU.mult))(fc, ri),
                               r=['rl%d' % ri], w=['aT%d' % fc])
                    else:
                        P.dve((lambda fc, ri: lambda e: e.tensor_tensor(out=aT[:, fc, :], in0=rl[ri], in1=rl[ri], op=ALU.mult))(fc, ri),
                              r=['rl%d' % ri], w=['aT%d' % fc])
                if J + 1 < NJ:
                    stats_norm(xts[1 - p], xsq, rs, rstd, hTs[1 - p], 1 - p, 8, xtok='xt6_%d' % (1 - p))
                for dc in range(8):
                    wi_ = w2_ctr % 2
                    w2_ctr += 1
                    P.dma('sp', 'ld_w2_%d' % wi_, (lambda dc, wi_: lambda e: e.dma_start(out=w2b[wi_], in_=W2[dc]))(dc, wi_), r=['W2'], w=['w2b%d' % wi_])
                    b = next_bank()
                    for fc in range(32):
                        P.pe((lambda fc, b, wi_: lambda e: e.matmul(banks[b][:], lhsT=w2b[wi_][:, fc, :], rhs=aT[:, fc, :],
                                                                    start=(fc == 0), stop=(fc == 31)))(fc, b, wi_), r=['w2b%d' % wi_, 'aT%d' % fc], w=['pb%d' % b])
                    P.dve((lambda dc, b, xt: lambda e: e.tensor_tensor(out=xt[:, dc, :], in0=xt[:, dc, :], in1=banks[b][:], op=ALU.add))(dc, b, xt),
                          r=['pb%d' % b, 'xt6_%d' % p], w=['x2_%d_%d' % (p, dc)])
                for ck in range(8):
                    P.act((lambda ck, xt: lambda e: e.activation(out=xsq[:, ck, :], in_=xt[:, ck, :], func=AF.Square))(ck, xt),
                          r=['x2_%d_%d' % (p, ck)], w=['xsq%d' % ck])
                b = next_bank()
                for ck in range(8):
                    P.pe((lambda ck, b: lambda e: e.matmul(banks[b][:], lhsT=ones[:], rhs=xsq[:, ck, :], start=(ck == 0), stop=(ck == 7)))(ck, b),
                         r=['xsq%d' % ck, 'ones'], w=['pb%d' % b])
                P.act((lambda b: lambda e: e.activation(out=rs, in_=banks[b][:], func=AF.Ln, scale=1.0 / D, bias=EPS))(b), r=['pb%d' % b], w=['rs'])
                P.act(lambda e: e.activation(out=rstd, in_=rs, func=AF.Exp, scale=-0.5), r=['rs'], w=['rstd'])
                for ck in range(8):
                    P.dve((lambda ck, xt: lambda e: e.scalar_tensor_tensor(out=xt[:, ck, :], in0=xt[:, ck, :], scalar=pv[:, 16 + ck:17 + ck], in1=rstd,
                                                                           op0=ALU.mult, op1=ALU.mult))(ck, xt), r=['x2_%d_%d' % (p, ck), 'rstd', 'pv'], w=['x3_%d_%d' % (p, ck)])
                P.dma('sp', 'out%d' % p, (lambda sl, xt: lambda e: e.dma_start(out=outT.rearrange("(c p) t -> p c t", p=128)[:, :, sl], in_=xt))(sl, xt),
                      r=['x3_%d_%d' % (p, ck) for ck in range(8)] + ['xt6_%d' % p], w=['xt6_%d' % p])

        phase6()

        P.emit(nc, final_waits=['out0', 'out1'])
    return nc


_NC_CACHE = {}
_DBG = {}


def _host_consts(c):
    cstv = np.zeros((128, NCST), np.float32)
    cstv[:, 0] = float(c)
    cstv[:, 1] = 1.0 - float(c)
    kk = np.arange(128, dtype=np.float64)
    for h in range(8):
        slope = 2.0 ** (-(h + 1))
        w = SUBW[h]
        for s in range(512 // w):
            q0 = (s + 1) * w - 1
            for delta in range(-56, 8):
                col = 2 + BIAS_BASE[h] + s * 64 + (delta + 56)
                if 128 * delta > 512 * c + q0:
                    cstv[:, col] = -30000.0
                else:
                    cstv[:, col] = (slope * (128 * delta + kk - 512 * c - q0)).astype(np.float32)
    m = np.zeros((128, 8, 512), np.float32)
    kkc = np.arange(128)[:, None]
    qq = np.arange(512)[None, :]
    for i in range(8):
        m[:, i, :] = np.where(128 * i + kkc <= 512 * c + qq, 0.0, -30000.0)
    return cstv, m.astype(ml_dtypes.bfloat16)


def kernel(x, w_in, b_gate, g_mix, lambda_q1, lambda_k1, lambda_q2, lambda_k2, subln_g,
           conv_w, conv_b, w_r, b_r, w_i, b_i, lru_lambda, w_att_out, w_rnn_out, w_o,
           g_mlp, w_ff1, w_ff2, g_final):
    f = lambda a: np.ascontiguousarray(np.asarray(a, dtype=np.float32))
    x = f(x)
    pvec = np.zeros((128, NPV), np.float32)
    col = lambda v: f(v).reshape(-1, 128).T
    pvec[:, 0:8] = col(g_mix[0])
    pvec[:, 8:16] = col(g_mlp[0])
    pvec[:, 16:24] = col(g_final)
    pvec[:, 24:40] = col(b_gate[0])
    for j in range(4):
        pvec[:, 40 + j * 8:48 + j * 8] = col(conv_w[0, j])
    pvec[:, 72:80] = col(conv_b[0])
    pvec[:, 80:88] = col(b_r[0])
    pvec[:, 88:96] = col(b_i[0])
    pvec[:, 96:104] = col(lru_lambda[0])
    pvec[:, 104] = f(subln_g[0])
    pvec[:, 105:169] = f(lambda_q1[0])[None, :]
    pvec[:, 169:233] = f(lambda_k1[0])[None, :]
    pvec[:, 233:297] = f(lambda_q2[0])[None, :]
    pvec[:, 297:361] = f(lambda_k2[0])[None, :]
    shared = {
        "w_in": f(w_in[0]), "w_r": f(w_r[0]), "w_i": f(w_i[0]), "w_att": f(w_att_out[0]), "w_rnn": f(w_rnn_out[0]),
        "w_o": f(w_o[0]), "w_ff1": f(w_ff1[0]), "w_ff2": f(w_ff2[0]), "pvec": pvec,
    }
    consts = [_host_consts(c) for c in range(2)]
    in_maps = []
    for core in range(8):
        p, c = core // 2, core % 2
        xT = np.ascontiguousarray(x[p].T)
        own = xT.reshape(D, 16, T)[:, c::2, :].reshape(D, NOWN)
        d = dict(shared)
        d["xT_seq"] = xT
        d["xT_own"] = np.ascontiguousarray(own)
        d["cst"] = consts[c][0]
        d["amask"] = consts[c][1]
        d["ident_in"] = np.eye(128, dtype=np.float32).astype(ml_dtypes.bfloat16)
        in_maps.append(d)
    if _DBG.get("on"):
        _DBG["in_maps"] = in_maps
        return None
    if "nc" not in _NC_CACHE:
        _NC_CACHE["nc"] = build()
    res = run_bass_kernel_spmd(_NC_CACHE["nc"], in_maps, core_ids=list(range(8)))
    out = np.empty((4, S, D), np.float32)
    for core in range(8):
        p, c = core // 2, core % 2
        oT = np.asarray(res.results[core]["outT"]).reshape(D, 8, T)
        o = oT.transpose(1, 2, 0)
        out[p].reshape(16, T, D)[c::2] = o
    return out
```

```python
import contextlib
import numpy as np
import ml_dtypes
import concourse.bass as bass
import concourse.mybir as mybir
from concourse.bass_utils import run_bass_kernel_spmd

F32 = mybir.dt.float32
BF16 = mybir.dt.bfloat16
AF = mybir.ActivationFunctionType
ALU = mybir.AluOpType
AX = mybir.AxisListType

S = 8192
D = 1024
NOWN = 4096
T = 512
EPS = 1e-6
NPV = 361
NCST = 770
SUBW = [128, 256, 512, 512, 512, 512, 512, 512]
WINB = [-(-int(150 * 2 ** (h + 1)) // 128) for h in range(8)]
BIAS_BASE = []
_b = 0
for _h in range(8):
    BIAS_BASE.append(_b)
    _b += (512 // SUBW[_h]) * 64
assert _b == 768


class Prog:
    EPOCH = 8000

    def __init__(self):
        self.ops = []
        self.bars = []

    def add(self, eng, fn, reads=(), writes=(), dma=None):
        self.ops.append((eng, fn, tuple(reads), tuple(writes), dma))

    def pe(self, fn, r=(), w=()):
        self.add('pe', fn, r, w)

    def act(self, fn, r=(), w=()):
        self.add('act', fn, r, w)

    def dve(self, fn, r=(), w=()):
        self.add('dve', fn, r, w)

    def pool(self, fn, r=(), w=()):
        self.add('pool', fn, r, w)

    def dma(self, q, semkey, fn, r=(), w=()):
        self.add(q, fn, r, w, dma=semkey)

    def barrier(self):
        self.bars.append(len(self.ops))

    def analyze(self):
        ops = self.ops
        n = len(ops)
        last_writer = {}
        readers = {}
        deps = [set() for _ in range(n)]
        eng_pos = [0] * n
        cnt = {}
        bars = set(self.bars)
        last_by_eng = {}
        last_by_dma = {}
        pending = {}
        for i, (eng, fn, rd, wr, dma) in enumerate(ops):
            if i in bars:
                snap = set(last_by_eng.values()) | set(last_by_dma.values())
                for e in ('pe', 'act', 'dve', 'pool', 'sp'):
                    pending[e] = snap
            if pending.get(eng) is not None:
                deps[i] |= pending[eng]
                pending[eng] = None
            eng_pos[i] = cnt.get(eng, 0)
            cnt[eng] = eng_pos[i] + 1
            for t in rd:
                if t in last_writer:
                    deps[i].add(last_writer[t])
            for t in wr:
                if t in last_writer:
                    deps[i].add(last_writer[t])
                for x in readers.get(t, {}).values():
                    if x != i:
                        deps[i].add(x)
            for t in wr:
                last_writer[t] = i
                readers[t] = {}
            for t in rd:
                if t not in wr:
                    readers.setdefault(t, {})[eng if dma is None else ('dma', dma)] = i
            if dma is None:
                last_by_eng[eng] = i
            else:
                last_by_dma[dma] = i
        signal = [False] * n
        for i in range(n):
            keep = set()
            eng = ops[i][0]
            for p in deps[i]:
                peng, pdma = ops[p][0], ops[p][4]
                if pdma is None and peng == eng and eng == 'pe':
                    continue
                keep.add(p)
                signal[p] = True
            deps[i] = keep
        sigval = [None] * n
        ccount = {}
        dcount = {}
        for i in range(n):
            eng, fn, rd, wr, dma = ops[i]
            if dma is not None:
                dcount[dma] = dcount.get(dma, 0) + 1
                sigval[i] = (('dma', dma), dcount[dma] * 16)
            elif signal[i]:
                c = ccount.get(eng, 0)
                ccount[eng] = c + 1
                sigval[i] = ((eng, c // self.EPOCH), c % self.EPOCH + 1)
        waited = {}
        waits = [None] * n
        for i in range(n):
            eng = ops[i][0]
            need = {}
            for p in deps[i]:
                k, v = sigval[p]
                if need.get(k, 0) < v:
                    need[k] = v
            w = waited.setdefault(eng, {})
            lst = []
            for k, v in need.items():
                if w.get(k, 0) >= v:
                    continue
                w[k] = v
                lst.append((k, v))
            waits[i] = lst
        self.sigval = sigval
        self.waits = waits
        keys = set()
        for s in sigval:
            if s is not None:
                keys.add(s[0])
        self.semkeys = sorted(keys, key=str)
        return self

    def emit(self, nc, final_waits=()):
        self.analyze()
        ops = self.ops
        with contextlib.ExitStack() as st:
            sems = {}
            for k in self.semkeys:
                sems[k] = st.enter_context(nc.semaphore("s_" + "_".join(str(x) for x in k)))
            block = st.enter_context(nc.Block())
            per_eng = {}
            for i, o in enumerate(ops):
                per_eng.setdefault(o[0], []).append(i)
            totals = {}
            for s in self.sigval:
                if s is not None and s[0][0] == 'dma':
                    totals[s[0]] = max(totals.get(s[0], 0), s[1])

            def run(engobj, name):
                for i in per_eng.get(name, ()):
                    for k, v in self.waits[i]:
                        engobj.wait_ge(sems[k], v)
                    inst = ops[i][1](engobj)
                    s = self.sigval[i]
                    if s is not None:
                        inst.then_inc(sems[s[0]], 16 if s[0][0] == 'dma' else 1)
                if name == 'sp':
                    for key in final_waits:
                        k = ('dma', key)
                        engobj.wait_ge(sems[k], totals[k])

            @block.sync
            def _(e):
                run(e, 'sp')

            @block.tensor
            def _(e):
                run(e, 'pe')

            @block.scalar
            def _(e):
                run(e, 'act')

            @block.vector
            def _(e):
                run(e, 'dve')

            @block.gpsimd
            def _(e):
                run(e, 'pool')


def build(dbg=False):
    nc = bass.Bass("TRN2", target_bir_lowering=False)

    def din(name, shape, dt=F32):
        return nc.dram_tensor(name, shape, dt, kind="ExternalInput").ap()

    def dscr(name, shape, dt):
        if dbg and name in ("KT", "VS", "QT", "AT", "RN", "GA", "GB", "X1"):
            return nc.dram_tensor(name, shape, dt, kind="ExternalOutput").ap()
        return nc.dram_tensor(name, shape, dt).ap()

    xT_seq = din("xT_seq", [D, S])
    xT_own = din("xT_own", [D, NOWN])
    w_in = din("w_in", [D, 7168])
    w_r = din("w_r", [8, 128, 128])
    w_i = din("w_i", [8, 128, 128])
    w_att = din("w_att", [D, D])
    w_rnn = din("w_rnn", [D, D])
    w_o = din("w_o", [D, D])
    w_ff1 = din("w_ff1", [D, 4096])
    w_ff2 = din("w_ff2", [4096, D])
    pvec = din("pvec", [128, NPV])
    cst = din("cst", [128, NCST])
    amask = din("amask", [128, 8, 512], BF16)
    ident_in = din("ident_in", [128, 128], BF16)
    outT = nc.dram_tensor("outT", [D, NOWN], F32, kind="ExternalOutput").ap()

    WIN = dscr("WIN", [128, 8, 7168], BF16)
    WATT = dscr("WATT", [128, 8, D], BF16)
    WRNN = dscr("WRNN", [128, 8, D], BF16)
    WO = dscr("WO", [128, 8, D], BF16)
    W1 = dscr("W1", [128, 8, 4096], BF16)
    W2 = dscr("W2", [8, 128, 32, 128], BF16)
    WR = dscr("WR", [128, 8, 128], BF16)
    WI = dscr("WI", [128, 8, 128], BF16)
    KT = dscr("KT", [8, 128, S], BF16)
    VS = dscr("VS", [8, 128, 64, 128], BF16)
    QT = dscr("QT", [8, 128, NOWN], BF16)
    AT = dscr("AT", [8, 128, NOWN], BF16)
    RN = dscr("RN", [8, 128, NOWN], BF16)
    GA = dscr("GA", [8, 128, NOWN], BF16)
    GB = dscr("GB", [8, 128, NOWN], BF16)
    X1 = dscr("X1", [8, 128, NOWN], F32)

    P = Prog()
    NBA = 99000
    with contextlib.ExitStack() as st:
        def sb(name, shape, dt):
            return st.enter_context(nc.sbuf_tensor(name, shape, dt))

        pv = sb("pv", [128, NPV], F32)
        cs = sb("cs", [128, NCST], F32)
        cc = sb("cc", [128, 32], F32)
        lt = sb("lt", [128, 128], F32)
        ones = sb("ones", [128, 128], BF16)
        wr_sb = sb("wr_sb", [128, 8, 128], BF16)
        wi_sb = sb("wi_sb", [128, 8, 128], BF16)
        saved = sb("saved", [128, 8, 16, 4], F32)
        sel = sb("sel", [128, 8, 8, 4], F32)
        halo = sb("halo", [128, 8, 3], F32)
        state = sb("state", [128, 8], F32)
        arena = sb("arena", [128, NBA], BF16)
        ident = sb("ident", [128, 128], BF16)
        psall = st.enter_context(nc.psum_tensor("psall", [128, 8 * 512], F32))
        banks = [psall[:, i * 512:(i + 1) * 512] for i in range(8)]

        class Arena:
            def __init__(self):
                self.o = 0

            def B(self, n):
                a = arena[:, self.o:self.o + n]
                self.o += (n + 15) // 16 * 16
                assert self.o <= NBA, self.o
                return a

            def F(self, n):
                a = arena[:, self.o:self.o + 2 * n].bitcast(F32)
                self.o += (2 * n + 15) // 16 * 16
                assert self.o <= NBA, self.o
                return a

        def c3(ap, c):
            return ap.rearrange("p (c t) -> p c t", c=c)

        NJ = NOWN // T
        bank_ctr = [0]

        def next_bank():
            i = bank_ctr[0] % 8
            bank_ctr[0] += 1
            return i

        MC1 = cs[:, 0:1]
        MC0 = cs[:, 1:2]

        P.dma('pool', 'cv_r', lambda e: e.dma_start(out=WR, in_=w_r.rearrange("n c d -> c n d")), w=['WR'])
        P.dma('pool', 'cv_i', lambda e: e.dma_start(out=WI, in_=w_i.rearrange("n c d -> c n d")), w=['WI'])
        pool_bg = []

        def cv_win(c0, c1, key, ck, step):
            return ('cv_' + key, (lambda e: e.dma_start(
                out=WIN[:, ck:ck + step, c0:c1],
                in_=w_in[ck * 128:(ck + step) * 128, c0:c1].rearrange("(c p) n -> p c n", p=128))), 'WIN' + key)

        for ck in range(0, 8, 2):
            k_, fn_, tok_ = cv_win(3072, 4096, 'x', ck, 2)
            P.dma('pool', k_, fn_, w=[tok_])
        for (c0, c1, key) in ((1024, 2048, 'k'), (2048, 3072, 'v')):
            for ck in range(0, 8, 2):
                pool_bg.append(cv_win(c0, c1, key, ck, 2))
        for ck in range(0, 8, 2):
            pool_bg.append(cv_win(0, 1024, 'a', ck, 2))
            pool_bg.append(cv_win(4096, 7168, 'a', ck, 2))

        def emit_bg(n):
            for _ in range(n):
                if pool_bg:
                    k_, fn_, tok_ = pool_bg.pop(0)
                    P.dma('pool', k_, fn_, w=[tok_])

        P.dma('sp', 'ld_pv', lambda e: e.dma_start(out=pv[:], in_=pvec), w=['pv'])
        P.dma('sp', 'ld_cs', lambda e: e.dma_start(out=cs[:], in_=cst), w=['cs'])
        P.dma('sp', 'ld_id', lambda e: e.dma_start(out=ident[:], in_=ident_in), w=['ident'])
        P.dve(lambda e: e.memset(ones[:], 1.0), w=['ones'])
        P.dve(lambda e: e.memset(halo[:], 0.0), w=['halo'])
        P.dve(lambda e: e.memset(state[:], 0.0), w=['state'])
        P.act(lambda e: e.activation(out=cc[:, 20:28], in_=pv[:, 96:104], func=AF.Exp, scale=-1.0), r=['pv'], w=['cc_t'])
        P.act(lambda e: e.activation(out=cc[:, 20:28], in_=cc[:, 20:28], func=AF.Ln, bias=1.0, scale=1.0), r=['cc_t'], w=['cc_t'])
        P.dve(lambda e: e.tensor_scalar_mul(out=cc[:, 0:8], in0=cc[:, 20:28], scalar1=-8.0), r=['cc_t'], w=['cc_ca'])
        P.dve(lambda e: e.tensor_scalar_mul(out=cc[:, 8:16], in0=cc[:, 20:28], scalar1=-16.0), r=['cc_t'], w=['cc_ca'])
        P.dve(lambda e: e.tensor_tensor(out=lt[:, 0:64], in0=pv[:, 105:169], in1=pv[:, 169:233], op=ALU.mult), r=['pv'], w=['lt'])
        P.dve(lambda e: e.tensor_tensor(out=lt[:, 64:128], in0=pv[:, 233:297], in1=pv[:, 297:361], op=ALU.mult), r=['pv'], w=['lt'])
        P.dve(lambda e: e.reduce_sum(out=cc[:, 28:29], in_=lt[:, 0:64], axis=AX.X), r=['lt'], w=['cc_l'])
        P.dve(lambda e: e.reduce_sum(out=cc[:, 29:30], in_=lt[:, 64:128], axis=AX.X), r=['lt'], w=['cc_l'])
        P.act(lambda e: e.activation(out=cc[:, 30:32], in_=cc[:, 28:30], func=AF.Exp), r=['cc_l'], w=['cc_l2'])
        P.dve(lambda e: e.tensor_tensor(out=cc[:, 18:19], in0=cc[:, 30:31], in1=cc[:, 31:32], op=ALU.subtract), r=['cc_l2'], w=['cc_l3'])
        P.dve(lambda e: e.tensor_scalar(out=cc[:, 16:17], in0=cc[:, 18:19], scalar1=-1.0, scalar2=-0.2, op0=ALU.mult, op1=ALU.add),
              r=['cc_l3'], w=['cc_lam'])
        P.dve(lambda e: e.tensor_scalar_mul(out=cc[:, 17:18], in0=pv[:, 104:105], scalar1=0.8), r=['pv'], w=['cc_gs'])
        NEGLAM = cc[:, 16:17]
        GS = cc[:, 17:18]

        def late_conversions():
            for (src, dst, key) in ((w_att, WATT, 'WATT'), (w_rnn, WRNN, 'WRNN'), (w_o, WO, 'WO')):
                for ck in range(0, 8, 4):
                    P.dma('pool', 'cv_' + key, (lambda ck, src, dst: lambda e: e.dma_start(
                        out=dst[:, ck:ck + 4, :],
                        in_=src[ck * 128:(ck + 4) * 128, :].rearrange("(c p) n -> p c n", p=128)))(ck, src, dst), w=[key])
            for ck in range(8):
                P.dma('pool', 'cv_f', (lambda ck: lambda e: e.dma_start(
                    out=W1[:, ck, :], in_=w_ff1[ck * 128:(ck + 1) * 128, :]))(ck), w=['W1'])
            for dc in range(8):
                P.dma('pool', 'cv_f', (lambda dc: lambda e: e.dma_start(
                    out=W2[dc], in_=w_ff2[:, dc * 128:(dc + 1) * 128].rearrange("(f p) j -> p f j", p=128)))(dc), w=['W2'])

        def load_x(xt, src, t):
            P.dma('sp', 'ld_x', lambda e: e.dma_start(
                out=xt, in_=src.rearrange("(c p) t -> p c t", p=128)[:, :, t * T:(t + 1) * T]), w=['xt'])

        def stats_norm(xt, xsq, rs, rstd, hT, hb, gcol, xtok='xt'):
            for ck in range(8):
                P.act((lambda ck: lambda e: e.activation(out=xsq[:, ck, :], in_=xt[:, ck, :], func=AF.Square))(ck),
                      r=[xtok], w=['xsq%d' % ck])
            b = next_bank()
            for ck in range(8):
                P.pe((lambda ck: lambda e: e.matmul(banks[b][:], lhsT=ones[:], rhs=xsq[:, ck, :], start=(ck == 0), stop=(ck == 7)))(ck),
                     r=['xsq%d' % ck, 'ones'], w=['pb%d' % b])
            P.act(lambda e: e.activation(out=rs, in_=banks[b][:], func=AF.Ln, scale=1.0 / D, bias=EPS), r=['pb%d' % b], w=['rs'])
            P.act(lambda e: e.activation(out=rstd, in_=rs, func=AF.Exp, scale=-0.5), r=['rs'], w=['rstd'])
            for ck in range(8):
                P.dve((lambda ck: lambda e: e.scalar_tensor_tensor(
                    out=hT[:, ck, :], in0=xt[:, ck, :], scalar=pv[:, gcol + ck:gcol + ck + 1], in1=rstd,
                    op0=ALU.mult, op1=ALU.mult))(ck), r=[xtok, 'rstd', 'pv'], w=['hT%d_%d' % (hb, ck)])

        evac_ctr = [0]

        def evac_copy(out_ap, b, wtok, in_ap=None, act_only=False):
            evac_ctr[0] += 1
            if in_ap is None:
                in_ap = banks[b][:]
            if act_only or evac_ctr[0] % 2 == 0:
                P.act(lambda e: e.copy(out=out_ap, in_=in_ap), r=['pb%d' % b], w=[wtok])
            else:
                P.dve(lambda e: e.tensor_copy(out=out_ap, in_=in_ap), r=['pb%d' % b], w=[wtok])

        def proj(b, wA, col0, hT, hb, ncol=128, wtok='wA'):
            for ck in range(8):
                P.pe((lambda ck: lambda e: e.matmul(banks[b][:], lhsT=wA[:, ck, col0:col0 + ncol], rhs=hT[:, ck, :],
                                                    start=(ck == 0), stop=(ck == 7)))(ck),
                     r=[wtok, 'hT%d_%d' % (hb, ck)], w=['pb%d' % b])

        class RnnPipe:
            def __init__(self, A, own, nsets, wA, xbcol0, gbcol0=None, rn_sb=None, nxb=8, wtok='wA'):
                self.own, self.wA, self.xbcol0, self.gbcol0, self.rn_sb, self.wtok = own, wA, xbcol0, gbcol0, rn_sb, wtok
                self.xbr = [A.F(520)[:, 0:515] for _ in range(nxb)]
                self.xctr = 0
                self.sets = []
                for i in range(nsets):
                    d = {k: A.F(512) for k in ('xc', 'rr', 'ii', 'aa')}
                    d['xcb'] = A.B(512)
                    if own:
                        d['gg'] = A.F(512)
                    d['id'] = i
                    self.sets.append(d)
                self.ctr = 0
                self.jobs = []
                self.after_first = None

            def add_tile(self, hT, hb, t=None, J=None, hook=None, post=(None, None)):
                for half in range(2):
                    ss = [self.sets[(self.ctr + i) % len(self.sets)] for i in range(4)]
                    self.ctr += 4
                    xbs = []
                    for i in range(4):
                        xi = self.xctr % len(self.xbr)
                        self.xctr += 1
                        xbs.append((self.xbr[xi], 'xb%d' % xi))
                    self.jobs.append(dict(hT=hT, hb=hb, t=t, J=J, ns=list(range(half * 4, half * 4 + 4)), ss=ss, xbs=xbs,
                                          hook=hook if half == 0 else None, post=post[half]))

            def run(self):
                nj = len(self.jobs)
                for k in range(nj + 2):
                    if 0 <= k - 2 < nj:
                        self.Z(self.jobs[k - 2])
                    if k < nj:
                        self.X(self.jobs[k])
                        if k == 0 and self.after_first is not None:
                            self.after_first()
                        else:
                            emit_bg(1)
                    if 0 <= k - 1 < nj:
                        self.Y(self.jobs[k - 1])
                    if k < nj:
                        self.X2(self.jobs[k])
                        if self.jobs[k]['post'] is not None:
                            self.jobs[k]['post']()

            def X(self, jb):
                own, t, J, hT, hb = self.own, jb['t'], jb['J'], jb['hT'], jb['hb']
                cw = lambda j, n: pv[:, 40 + j * 8 + n:41 + j * 8 + n]
                for n, (xb, xtok) in zip(jb['ns'], jb['xbs']):
                    if own:
                        P.pool((lambda xb, n: lambda e: e.tensor_copy(out=xb[:, 0:3], in_=sel[:, n, J, 0:3]))(xb, n), r=['sel'], w=[xtok])
                    else:
                        P.pool((lambda xb, n: lambda e: e.tensor_copy(out=xb[:, 0:3], in_=halo[:, n, :]))(xb, n), r=['halo%d' % n, 'halo'], w=[xtok])
                    b = next_bank()
                    proj(b, self.wA, self.xbcol0 + n * 128, hT, hb, wtok=self.wtok)
                    P.act((lambda xb, b: lambda e: e.copy(out=xb[:, 3:515], in_=banks[b][:]))(xb, b), r=['pb%d' % b], w=[xtok])
                    if not own:
                        P.pool((lambda xb, n: lambda e: e.tensor_copy(out=halo[:, n, :], in_=xb[:, 512:515]))(xb, n), r=[xtok], w=['halo%d' % n])
                        P.pool((lambda xb, n: lambda e: e.tensor_copy(out=saved[:, n, t, 0:3], in_=xb[:, 512:515]))(xb, n), r=[xtok], w=['saved'])
                for n, d, (xb, xtok) in zip(jb['ns'], jb['ss'], jb['xbs']):
                    xc, ct, i = d['xc'], d['aa'], d['id']
                    P.pool((lambda xb, xc, n: lambda e: e.tensor_scalar(out=xc, in0=xb[:, 3:515], scalar1=cw(3, n), scalar2=pv[:, 72 + n:73 + n],
                                                                        op0=ALU.mult, op1=ALU.add))(xb, xc, n), r=[xtok, 'pv'], w=['xc_%d' % i])
                    P.pool((lambda xb, ct, n: lambda e: e.tensor_scalar(out=ct, in0=xb[:, 2:514], scalar1=cw(2, n), scalar2=0.0,
                                                                        op0=ALU.mult, op1=ALU.add))(xb, ct, n), r=[xtok, 'pv'], w=['aa_%d' % i])
                    P.pool((lambda xc, ct: lambda e: e.tensor_tensor(out=xc, in0=xc, in1=ct, op=ALU.add))(xc, ct), r=['aa_%d' % i, 'xc_%d' % i], w=['xc_%d' % i])
                for n, d, (xb, xtok) in zip(jb['ns'], jb['ss'], jb['xbs']):
                    xc, i = d['xc'], d['id']
                    for j in (1, 0):
                        P.dve((lambda xb, xc, j, n: lambda e: e.scalar_tensor_tensor(out=xc, in0=xb[:, j:j + 512], scalar=cw(j, n), in1=xc,
                                                                                     op0=ALU.mult, op1=ALU.add))(xb, xc, j, n),
                              r=[xtok, 'pv', 'xc_%d' % i], w=['xc_%d' % i])

            def X2(self, jb):
                for n, d in zip(jb['ns'], jb['ss']):
                    P.act((lambda xc, xcb: lambda e: e.copy(out=xcb, in_=xc))(d['xc'], d['xcb']), r=['xc_%d' % d['id']], w=['xcb_%d' % d['id']])

            def Y(self, jb):
                own, hT, hb = self.own, jb['hT'], jb['hb']
                for n, d in zip(jb['ns'], jb['ss']):
                    xcb, i = d['xcb'], d['id']
                    br = next_bank()
                    P.pe((lambda br, n, xcb: lambda e: e.matmul(banks[br][:], lhsT=wr_sb[:, n, :], rhs=xcb, start=True, stop=True))(br, n, xcb),
                         r=['xcb_%d' % i, 'wr_sb'], w=['pb%d' % br])
                    P.act((lambda br, n, rr: lambda e: e.activation(out=rr, in_=banks[br][:], func=AF.Sigmoid, bias=pv[:, 80 + n:81 + n]))(br, n, d['rr']),
                          r=['pb%d' % br, 'pv'], w=['rr_%d' % i])
                    bi = next_bank()
                    P.pe((lambda bi, n, xcb: lambda e: e.matmul(banks[bi][:], lhsT=wi_sb[:, n, :], rhs=xcb, start=True, stop=True))(bi, n, xcb),
                         r=['xcb_%d' % i, 'wi_sb'], w=['pb%d' % bi])
                    P.act((lambda bi, n, ii: lambda e: e.activation(out=ii, in_=banks[bi][:], func=AF.Sigmoid, bias=pv[:, 88 + n:89 + n]))(bi, n, d['ii']),
                          r=['pb%d' % bi, 'pv'], w=['ii_%d' % i])
                for n, d in zip(jb['ns'], jb['ss']):
                    rr, aa, i = d['rr'], d['aa'], d['id']
                    P.act((lambda rr, aa, n: lambda e: e.activation(out=aa, in_=rr, func=AF.Exp, scale=cc[:, n:n + 1]))(rr, aa, n),
                          r=['rr_%d' % i, 'cc_ca'], w=['aa_%d' % i])
                    P.act((lambda rr, n: lambda e: e.activation(out=rr, in_=rr, func=AF.Exp, scale=cc[:, 8 + n:9 + n]))(rr, n),
                          r=['rr_%d' % i, 'cc_ca'], w=['rr_%d' % i])
                if jb['hook'] is not None:
                    jb['hook']()
                for n, d in zip(jb['ns'], jb['ss']):
                    rr, i = d['rr'], d['id']
                    P.act((lambda rr: lambda e: e.activation(out=rr, in_=rr, func=AF.Sqrt, scale=-1.0, bias=1.0))(rr), r=['rr_%d' % i], w=['rr_%d' % i])
                if own:
                    for n, d in zip(jb['ns'], jb['ss']):
                        bg = next_bank()
                        proj(bg, self.wA, self.gbcol0 + n * 128, hT, hb, wtok=self.wtok)
                        P.act((lambda gg, bg: lambda e: e.activation(out=gg, in_=banks[bg][:], func=AF.Gelu_apprx_tanh))(d['gg'], bg),
                              r=['pb%d' % bg], w=['gg_%d' % d['id']])

            def Z(self, jb):
                own, t, J = self.own, jb['t'], jb['J']
                for n, d in zip(jb['ns'], jb['ss']):
                    xc, rr, ii, aa, i = d['xc'], d['rr'], d['ii'], d['aa'], d['id']
                    if not own and t == 0:
                        P.dve((lambda rr: lambda e: e.memset(rr[:, 0:1], 1.0))(rr), r=['rr_%d' % i], w=['rr_%d' % i])
                    elif own and J == 0:
                        P.dve((lambda rr: lambda e: e.tensor_scalar(out=rr[:, 0:1], in0=rr[:, 0:1], scalar1=MC1, scalar2=MC0,
                                                                    op0=ALU.mult, op1=ALU.add))(rr), r=['rr_%d' % i, 'cs'], w=['rr_%d' % i])
                    P.dve((lambda ii, xc: lambda e: e.tensor_tensor(out=ii, in0=ii, in1=xc, op=ALU.mult))(ii, xc), r=['ii_%d' % i, 'xc_%d' % i], w=['ii_%d' % i])
                    P.dve((lambda ii, rr: lambda e: e.tensor_tensor(out=ii, in0=ii, in1=rr, op=ALU.mult))(ii, rr), r=['ii_%d' % i, 'rr_%d' % i], w=['ii_%d' % i])
                    if own:
                        init_ap, init_tok = sel[:, n, J, 3:4], ['sel']
                    else:
                        init_ap, init_tok = state[:, n:n + 1], ['state%d' % n, 'state']
                    P.dve((lambda xc, aa, ii, init_ap: lambda e: e.tensor_tensor_scan(out=xc, data0=aa, data1=ii, initial=init_ap,
                                                                                      op0=ALU.mult, op1=ALU.add))(xc, aa, ii, init_ap),
                          r=['aa_%d' % i, 'ii_%d' % i] + init_tok, w=['xc_%d' % i])
                    if own:
                        P.dve((lambda xc, gg, n: lambda e: e.tensor_tensor(out=self.rn_sb[:, n, :], in0=xc, in1=gg, op=ALU.mult))(xc, d['gg'], n),
                              r=['xc_%d' % i, 'gg_%d' % i], w=['rn_sb%d' % n])
                    else:
                        P.dve((lambda xc, n: lambda e: e.tensor_copy(out=state[:, n:n + 1], in_=xc[:, 511:512]))(xc, n), r=['xc_%d' % i, 'state'], w=['state%d' % n])
                        P.dve((lambda xc, n: lambda e: e.tensor_copy(out=saved[:, n, t, 3:4], in_=xc[:, 511:512]))(xc, n), r=['xc_%d' % i], w=['saved_s'])
                if own and jb['ns'][0] == 4:
                    P.dma('sp', 'st_rn', (lambda J: lambda e: e.dma_start(
                        out=RN[:, :, J * T:(J + 1) * T].rearrange("h p t -> p h t"), in_=self.rn_sb))(J), r=['rn_sb%d' % q for q in range(8)])

        NT = S // T

        def phase1():
            A = Arena()
            wA = c3(A.B(8 * 3072), 8)
            xsq = c3(A.B(8 * T), 8)
            hTs = [c3(A.B(8 * T), 8) for _ in range(2)]
            kt_sb = c3(A.B(8 * T), 8)
            v_sb = A.B(8 * 4 * 128).rearrange("p (h b e) -> p h b e", h=8, b=4)
            xt = c3(A.F(8 * T), 8)
            rs = A.F(T)
            rstd = A.F(T)
            pipe = RnnPipe(A, False, 8, wA, 2048, nxb=6, wtok='wAx')
            load_x(xt, xT_seq, 0)
            P.dma('sp', 'ld_wr', lambda e: e.dma_start(out=wr_sb[:], in_=WR), r=['WR'], w=['wr_sb'])
            P.dma('sp', 'ld_wi', lambda e: e.dma_start(out=wi_sb[:], in_=WI), r=['WI'], w=['wi_sb'])

            def load_w(c0, key):
                for ck in range(0, 8, 4):
                    P.dma('sp', 'ld_wA' + key, (lambda ck: lambda e: e.dma_start(out=wA[:, ck:ck + 4, c0 - 1024:c0], in_=WIN[:, ck:ck + 4, c0:c0 + 1024]))(ck),
                          r=['WIN' + key], w=['wA' + key])

            load_w(3072, 'x')

            def after_first():
                emit_bg(8)
                load_w(1024, 'k')
                load_w(2048, 'v')
            pipe.after_first = after_first

            def k_part(t, hT, hb):
                for h in range(8):
                    b = next_bank()
                    proj(b, wA, h * 128, hT, hb, wtok='wAk')
                    evac_copy(kt_sb[:, h, :], b, 'kt_sb%d' % h, act_only=True)
                P.dma('sp', 'st_kt', lambda e: e.dma_start(
                    out=KT[:, :, t * T:(t + 1) * T].rearrange("h p t -> p h t"), in_=kt_sb), r=['kt_sb%d' % i for i in range(8)])

            def v_part(t, hT, hb):
                for j in range(4):
                    for half in range(2):
                        b = next_bank()
                        for ck in range(8):
                            P.pe((lambda ck, j, half, b: lambda e: e.matmul(
                                banks[b][:], lhsT=hT[:, ck, j * 128:(j + 1) * 128], rhs=wA[:, ck, 1024 + half * 512:1536 + half * 512],
                                start=(ck == 0), stop=(ck == 7)))(ck, j, half, b), r=['wAv', 'hT%d_%d' % (hb, ck)], w=['pb%d' % b])
                        evac_copy(v_sb[:, half * 4:(half + 1) * 4, j, :], b, 'v_sb%d' % (j * 2 + half),
                                  in_ap=banks[b][:].rearrange("p (h e) -> p h e", h=4))
                P.dma('sp', 'st_v', lambda e: e.dma_start(
                    out=VS[:, :, 4 * t:4 * t + 4, :].rearrange("h p b e -> p h b e"), in_=v_sb), r=['v_sb%d' % i for i in range(8)])

            stats_norm(xt, xsq, rs, rstd, hTs[0], 0, 0)
            for t in range(NT):
                hb = t % 2
                hook = None
                if t + 1 < NT:
                    hook = (lambda hb, t: lambda: (load_x(xt, xT_seq, t + 1), stats_norm(xt, xsq, rs, rstd, hTs[1 - hb], 1 - hb, 0)))(hb, t)
                post = ((lambda t, hb: lambda: k_part(t, hTs[hb], hb))(t, hb), (lambda t, hb: lambda: v_part(t, hTs[hb], hb))(t, hb))
                pipe.add_tile(hTs[hb], hb, t=t, hook=hook, post=post)
            pipe.run()
            emit_bg(100)
        phase1()

        P.barrier()

        def phase2():
            A = Arena()
            wA = c3(A.B(8 * 3072), 8)
            xsq = c3(A.B(8 * T), 8)
            hTs = [c3(A.B(8 * T), 8) for _ in range(2)]
            qt_sb = c3(A.B(8 * T), 8)
            g_sb = c3(A.B(16 * T), 16)
            xt = c3(A.F(8 * T), 8)
            rs = A.F(T)
            rstd = A.F(T)
            load_x(xt, xT_own, 0)
            for ck in range(0, 8, 4):
                P.dma('sp', 'ld_wAq', (lambda ck: lambda e: e.dma_start(out=wA[:, ck:ck + 4, 0:1024], in_=WIN[:, ck:ck + 4, 0:1024]))(ck), r=['WINa'], w=['wAq'])
            for ck in range(0, 8, 2):
                P.dma('sp', 'ld_wAg', (lambda ck: lambda e: e.dma_start(out=wA[:, ck:ck + 2, 1024:3072], in_=WIN[:, ck:ck + 2, 5120:7168]))(ck), r=['WINa'], w=['wAg'])
            stats_norm(xt, xsq, rs, rstd, hTs[0], 0, 0)
            for J in range(NJ):
                hb = J % 2
                hT = hTs[hb]
                if J + 1 < NJ:
                    load_x(xt, xT_own, J + 1)
                for h in range(8):
                    b = next_bank()
                    proj(b, wA, h * 128, hT, hb, wtok='wAq')
                    evac_copy(qt_sb[:, h, :], b, 'qt_sb%d' % h)
                P.dma('sp', 'st_q', (lambda J: lambda e: e.dma_start(
                    out=QT[:, :, J * T:(J + 1) * T].rearrange("h p t -> p h t"), in_=qt_sb))(J), r=['qt_sb%d' % i for i in range(8)])
                if J + 1 < NJ:
                    stats_norm(xt, xsq, rs, rstd, hTs[1 - hb], 1 - hb, 0)
                for gch in range(16):
                    b = next_bank()
                    proj(b, wA, 1024 + gch * 128, hT, hb, wtok='wAg')
                    P.act((lambda gch, b: lambda e: e.activation(out=g_sb[:, gch, :], in_=banks[b][:], func=AF.Sigmoid,
                                                                 bias=pv[:, 24 + gch:25 + gch]))(gch, b), r=['pb%d' % b, 'pv'], w=['g_sb%d' % gch])
                P.dma('sp', 'st_ga', (lambda J: lambda e: e.dma_start(
                    out=GA[:, :, J * T:(J + 1) * T].rearrange("h p t -> p h t"), in_=g_sb[:, 0:8, :]))(J), r=['g_sb%d' % i for i in range(8)])
                P.dma('sp', 'st_gb', (lambda J: lambda e: e.dma_start(
                    out=GB[:, :, J * T:(J + 1) * T].rearrange("h p t -> p h t"), in_=g_sb[:, 8:16, :]))(J), r=['g_sb%d' % i for i in range(8, 16)])
        phase2()

        P.barrier()

        def phase3():
            A = Arena()
            wA = c3(A.B(8 * 2048), 8)
            xsq = c3(A.B(8 * T), 8)
            hTs = [c3(A.B(8 * T), 8) for _ in range(2)]
            rn_sb = c3(A.B(8 * T), 8)
            xt = c3(A.F(8 * T), 8)
            rs = A.F(T)
            rstd = A.F(T)
            pipe = RnnPipe(A, True, 8, wA, 0, gbcol0=1024, rn_sb=rn_sb)
            for ck in range(8):
                P.dma('sp', 'ld_wA', (lambda ck: lambda e: e.dma_start(out=wA[:, ck, :], in_=WIN[:, ck, 3072:5120]))(ck), r=['WINx', 'WINa'], w=['wA'])
            for J in range(NJ):
                P.dve((lambda J: lambda e: e.tensor_scalar_mul(out=sel[:, :, J, :], in0=saved[:, :, 2 * J, :], scalar1=MC1))(J),
                      r=['saved', 'saved_s', 'cs'], w=['sel'])
                if J >= 1:
                    P.dve((lambda J: lambda e: e.scalar_tensor_tensor(out=sel[:, :, J, :], in0=saved[:, :, 2 * J - 1, :], scalar=MC0,
                                                                      in1=sel[:, :, J, :], op0=ALU.mult, op1=ALU.add))(J),
                          r=['saved', 'saved_s', 'cs', 'sel'], w=['sel'])
            load_x(xt, xT_own, 0)
            stats_norm(xt, xsq, rs, rstd, hTs[0], 0, 0)
            for J in range(NJ):
                hb = J % 2
                hook = None
                if J + 1 < NJ:
                    hook = (lambda hb, J: lambda: (load_x(xt, xT_own, J + 1), stats_norm(xt, xsq, rs, rstd, hTs[1 - hb], 1 - hb, 0)))(hb, J)
                pipe.add_tile(hTs[hb], hb, J=J, hook=hook)
            pipe.run()
        phase3()

        def phase4():
            P.barrier()
            A = Arena()
            KTh = [A.B(S) for _ in range(2)]
            Vh = [A.B(S).rearrange("p (b e) -> p b e", e=128) for _ in range(2)]
            Qh = [A.B(NOWN) for _ in range(2)]
            am_sb = c3(A.B(8 * 512), 8)
            NPB = 3
            Pb2 = [A.B(1024) for _ in range(NPB)]
            Pb = [[Pb2[i][:, m * 512:(m + 1) * 512] for i in range(NPB)] for m in range(2)]
            osq = A.B(512)
            att_sb = [A.B(512) for _ in range(2)]
            r1 = A.F(512)
            r2 = A.F(512)
            t1 = A.F(512)
            t2 = A.F(512)
            oos = [A.F(512) for _ in range(2)]
            rsa = A.F(512)
            rstda = A.F(512)
            P.dma('sp', 'ld_c2', lambda e: e.dma_start(out=am_sb, in_=amask), w=['am_sb'])
            late_conversions()
            SB = [0, 1, 2, 3]
            BO1, BO2, BL1, BL2 = 4, 5, 6, 7
            s_ctr = 0
            p_ctr = 0
            pending_epi = [None]

            def load_head(h):
                hbuf = h % 2
                P.dma('sp', 'ld_kt%d' % hbuf, lambda e: e.dma_start(out=KTh[hbuf], in_=KT[h]), w=['KTh%d' % hbuf])
                P.dma('sp', 'ld_v%d' % hbuf, lambda e: e.dma_start(out=Vh[hbuf], in_=VS[h]), w=['Vh%d' % hbuf])
                P.dma('sp', 'ld_q%d' % hbuf, lambda e: e.dma_start(out=Qh[hbuf], in_=QT[h]), w=['Qh%d' % hbuf])

            def emit_qk(h, J, kb, slot):
                hbuf = h % 2
                delta = kb - 8 * J
                for m in range(2):
                    b = 2 * slot + m
                    P.pe((lambda m, b: lambda e: e.matmul(
                        banks[b][:], lhsT=KTh[hbuf][64 * m:64 * m + 64, kb * 128:(kb + 1) * 128],
                        rhs=Qh[hbuf][64 * m:64 * m + 64, J * T:(J + 1) * T], start=True, stop=(delta < 0)))(m, b),
                        r=['KTh%d' % hbuf, 'Qh%d' % hbuf], w=['pb%d' % b])
                if delta >= 0:
                    for m in range(2):
                        b = 2 * slot + m
                        P.pe((lambda b: lambda e: e.matmul(banks[b][:], lhsT=ident[:], rhs=am_sb[:, delta, :], start=False, stop=True))(b),
                             r=['ident', 'am_sb'], w=['pb%d' % b])

            def emit_exp(h, J, kb, slot, pi):
                w = SUBW[h]
                delta = kb - 8 * J
                for m in range(2):
                    b = 2 * slot + m
                    for s in range(512 // w):
                        col = 2 + BIAS_BASE[h] + s * 64 + (delta + 56)
                        P.act((lambda m, b, s, col: lambda e: e.activation(
                            out=Pb[m][pi][:, s * w:(s + 1) * w], in_=banks[b][:, s * w:(s + 1) * w], func=AF.Exp,
                            bias=cs[:, col:col + 1], scale=0.125))(m, b, s, col), r=['pb%d' % b, 'cs'], w=['P%d_%d' % (m, pi)])

            def emit_pv(h, J, kb, first, last, pi):
                hbuf = h % 2
                for m in range(2):
                    bo = (BO1, BO2)[m]
                    bl = (BL1, BL2)[m]
                    P.pe((lambda m, bo: lambda e: e.matmul(banks[bo][:], lhsT=Vh[hbuf][:, kb, :], rhs=Pb[m][pi],
                                                           start=first, stop=last))(m, bo),
                         r=['Vh%d' % hbuf, 'P%d_%d' % (m, pi)], w=['pb%d' % bo])
                    P.pe((lambda m, bl: lambda e: e.matmul(banks[bl][:], lhsT=ones[:], rhs=Pb[m][pi],
                                                           start=first, stop=last))(m, bl),
                         r=['ones', 'P%d_%d' % (m, pi)], w=['pb%d' % bl])

            epi_ctr = [0]

            def epi_part1(h, J):
                ei = epi_ctr[0] % 2
                epi_ctr[0] += 1
                oo = oos[ei]
                P.dve(lambda e: e.tensor_copy(out=t1, in_=banks[BO1][:]), r=['pb%d' % BO1], w=['t1'])
                P.dve(lambda e: e.tensor_copy(out=r1, in_=banks[BL1][:]), r=['pb%d' % BL1], w=['r1'])
                P.dve(lambda e: e.tensor_copy(out=t2, in_=banks[BO2][:]), r=['pb%d' % BO2], w=['t2'])
                P.dve(lambda e: e.tensor_copy(out=r2, in_=banks[BL2][:]), r=['pb%d' % BL2], w=['r2'])
                P.dve(lambda e: e.reciprocal(out=r1, in_=r1), r=['r1'], w=['r1'])
                P.dve(lambda e: e.reciprocal(out=r2, in_=r2), r=['r2'], w=['r2'])
                P.dve(lambda e: e.tensor_tensor(out=t1, in0=t1, in1=r1, op=ALU.mult), r=['t1', 'r1'], w=['t1'])
                P.dve(lambda e: e.tensor_tensor(out=t2, in0=t2, in1=r2, op=ALU.mult), r=['t2', 'r2'], w=['t2'])
                P.dve(lambda e: e.scalar_tensor_tensor(out=oo, in0=t2, scalar=NEGLAM, in1=t1, op0=ALU.mult, op1=ALU.add),
                      r=['t1', 't2', 'cc_lam'], w=['oo%d' % ei])
                P.dve(lambda e: e.tensor_tensor(out=osq, in0=oo, in1=oo, op=ALU.mult), r=['oo%d' % ei], w=['osq'])
                return (h, J, ei)

            def epi_part2(args, slot):
                h, J, ei = args
                oo = oos[ei]
                b = 2 * slot
                asb = att_sb[ei]
                P.pe(lambda e: e.matmul(banks[b][:], lhsT=ones[:], rhs=osq, start=True, stop=True), r=['osq', 'ones'], w=['pb%d' % b])
                P.act(lambda e: e.activation(out=rsa, in_=banks[b][:], func=AF.Ln, scale=1.0 / 128, bias=EPS), r=['pb%d' % b], w=['rsa'])
                P.act(lambda e: e.activation(out=rstda, in_=rsa, func=AF.Exp, scale=-0.5), r=['rsa'], w=['rstda'])
                P.dve(lambda e: e.scalar_tensor_tensor(out=asb, in0=oo, scalar=GS, in1=rstda, op0=ALU.mult, op1=ALU.mult),
                      r=['oo%d' % ei, 'rstda', 'cc_gs'], w=['att%d' % ei])
                P.dma('sp', 'st_at%d' % ei, lambda e: e.dma_start(out=AT[h][:, J * T:(J + 1) * T], in_=asb), r=['att%d' % ei])

            tiles = []
            for h in range(8):
                for J in range(NJ):
                    kbs = list(range(max(0, 8 * J - WINB[h]), 8 * J + 8))
                    for i, kb in enumerate(kbs):
                        tiles.append((h, J, kb, i, len(kbs)))
            NTI = len(tiles)
            load_head(0)
            load_head(1)
            for g in (0, 1):
                emit_qk(tiles[g][0], tiles[g][1], tiles[g][2], g % 2)
            last_slot = 0
            for g, (h, J, kb, i, n) in enumerate(tiles):
                if i == 0 and J == 0 and 1 <= h < 7:
                    load_head(h + 1)
                pi = p_ctr % NPB
                p_ctr += 1
                emit_exp(h, J, kb, g % 2, pi)
                if i == min(10, n - 1) and pending_epi[0] is not None:
                    epi_part2(pending_epi[0], g % 2)
                    pending_epi[0] = None
                if g + 2 < NTI:
                    h2, J2, kb2, _, _ = tiles[g + 2]
                    emit_qk(h2, J2, kb2, g % 2)
                emit_pv(h, J, kb, i == 0, i == n - 1, pi)
                if i == n - 1:
                    pending_epi[0] = epi_part1(h, J)
                last_slot = g % 2
            epi_part2(pending_epi[0], last_slot)

        phase4()

        def phase5():
            P.barrier()
            A = Arena()
            watt = c3(A.B(8 * D), 8)
            wrnn = c3(A.B(8 * D), 8)
            wo = c3(A.B(8 * D), 8)
            at_sb = [c3(A.B(8 * T), 8) for _ in range(2)]
            rn_in = [c3(A.B(8 * T), 8) for _ in range(2)]
            ga_sb = [c3(A.B(8 * T), 8) for _ in range(2)]
            gb_sb = [c3(A.B(8 * T), 8) for _ in range(2)]
            mT = c3(A.B(8 * T), 8)
            xts = [c3(A.F(8 * T), 8) for _ in range(2)]
            m1 = [A.F(T) for _ in range(2)]
            m2 = [A.F(T) for _ in range(2)]
            P.dma('sp', 'ld_watt', lambda e: e.dma_start(out=watt, in_=WATT), r=['WATT'], w=['watt'])
            P.dma('sp', 'ld_wrnn', lambda e: e.dma_start(out=wrnn, in_=WRNN), r=['WRNN'], w=['wrnn'])

            def loads(J):
                p = J % 2
                sl = slice(J * T, (J + 1) * T)
                P.dma('sp', 'ld_at%d' % p, lambda e: e.dma_start(out=at_sb[p], in_=AT[:, :, sl].rearrange("h p t -> p h t")), w=['at_sb%d' % p])
                P.dma('sp', 'ld_rn%d' % p, lambda e: e.dma_start(out=rn_in[p], in_=RN[:, :, sl].rearrange("h p t -> p h t")), w=['rn_in%d' % p])
                P.dma('sp', 'ld_ga%d' % p, lambda e: e.dma_start(out=ga_sb[p], in_=GA[:, :, sl].rearrange("h p t -> p h t")), w=['ga_sb%d' % p])
                P.dma('sp', 'ld_gb%d' % p, lambda e: e.dma_start(out=gb_sb[p], in_=GB[:, :, sl].rearrange("h p t -> p h t")), w=['gb_sb%d' % p])
                P.dma('sp', 'ld_x5_%d' % p, lambda e: e.dma_start(
                    out=xts[p], in_=xT_own.rearrange("(c p) t -> p c t", p=128)[:, :, sl]), w=['xt5_%d' % p])

            loads(0)
            P.dma('sp', 'ld_wo', lambda e: e.dma_start(out=wo, in_=WO), r=['WO'], w=['wo'])
            for J in range(NJ):
                p = J % 2
                sl = slice(J * T, (J + 1) * T)
                xt = xts[p]
                if J + 1 < NJ:
                    loads(J + 1)
                for dc in range(8):
                    ba = next_bank()
                    for ck in range(8):
                        P.pe((lambda ck, dc, ba, p: lambda e: e.matmul(banks[ba][:], lhsT=watt[:, ck, dc * 128:(dc + 1) * 128], rhs=at_sb[p][:, ck, :],
                                                                       start=(ck == 0), stop=(ck == 7)))(ck, dc, ba, p), r=['watt', 'at_sb%d' % p], w=['pb%d' % ba])
                    bb = next_bank()
                    for ck in range(8):
                        P.pe((lambda ck, dc, bb, p: lambda e: e.matmul(banks[bb][:], lhsT=wrnn[:, ck, dc * 128:(dc + 1) * 128], rhs=rn_in[p][:, ck, :],
                                                                       start=(ck == 0), stop=(ck == 7)))(ck, dc, bb, p), r=['wrnn', 'rn_in%d' % p], w=['pb%d' % bb])
                    mi = dc % 2
                    P.dve((lambda dc, ba, mi, p: lambda e: e.tensor_tensor(out=m1[mi], in0=banks[ba][:], in1=ga_sb[p][:, dc, :], op=ALU.mult))(dc, ba, mi, p),
                          r=['pb%d' % ba, 'ga_sb%d' % p], w=['m1_%d' % mi])
                    P.dve((lambda dc, bb, mi, p: lambda e: e.tensor_tensor(out=m2[mi], in0=banks[bb][:], in1=gb_sb[p][:, dc, :], op=ALU.mult))(dc, bb, mi, p),
                          r=['pb%d' % bb, 'gb_sb%d' % p], w=['m2_%d' % mi])
                    P.pool((lambda dc, mi: lambda e: e.tensor_tensor(out=mT[:, dc, :], in0=m1[mi], in1=m2[mi], op=ALU.add))(dc, mi),
                           r=['m1_%d' % mi, 'm2_%d' % mi], w=['mT%d' % dc])
                for dc in range(8):
                    b = next_bank()
                    for ck in range(8):
                        P.pe((lambda ck, dc, b: lambda e: e.matmul(banks[b][:], lhsT=wo[:, ck, dc * 128:(dc + 1) * 128], rhs=mT[:, ck, :],
                                                                   start=(ck == 0), stop=(ck == 7)))(ck, dc, b), r=['wo', 'mT%d' % ck], w=['pb%d' % b])
                    P.dve((lambda dc, b, xt: lambda e: e.tensor_tensor(out=xt[:, dc, :], in0=xt[:, dc, :], in1=banks[b][:], op=ALU.add))(dc, b, xt),
                          r=['pb%d' % b, 'xt5_%d' % p], w=['x1_%d_%d' % (p, dc)])
                P.dma('sp', 'st_x1_%d' % p, (lambda sl, xt: lambda e: e.dma_start(out=X1[:, :, sl].rearrange("h p t -> p h t"), in_=xt))(sl, xt),
                      r=['x1_%d_%d' % (p, dc) for dc in range(8)] + ['xt5_%d' % p], w=['xt5_%d' % p])

        phase5()

        def phase6():
            P.barrier()
            A = Arena()
            w1 = c3(A.B(8 * 4096), 8)
            w2b = [A.B(32 * 128).rearrange("p (f j) -> p f j", j=128) for _ in range(2)]
            xsq = c3(A.B(8 * T), 8)
            hTs = [c3(A.B(8 * T), 8) for _ in range(2)]
            aT = c3(A.B(32 * T), 32)
            xts = [c3(A.F(8 * T), 8) for _ in range(2)]
            rs = A.F(T)
            rstd = A.F(T)
            rl = [A.F(T) for _ in range(2)]

            def load6(J):
                p = J % 2
                sl = slice(J * T, (J + 1) * T)
                P.dma('sp', 'ld_x6_%d' % p, lambda e: e.dma_start(out=xts[p], in_=X1[:, :, sl].rearrange("h p t -> p h t")), w=['xt6_%d' % p])

            load6(0)
            for q4 in range(4):
                P.dma('sp', 'ld_w1_%d' % q4, (lambda q4: lambda e: e.dma_start(out=w1[:, :, q4 * 1024:(q4 + 1) * 1024], in_=W1[:, :, q4 * 1024:(q4 + 1) * 1024]))(q4),
                      r=['W1'], w=['w1_%d' % q4])
            stats_norm(xts[0], xsq, rs, rstd, hTs[0], 0, 8, xtok='xt6_0')
            w2_ctr = 0
            for J in range(NJ):
                p = J % 2
                sl = slice(J * T, (J + 1) * T)
                xt, hT = xts[p], hTs[p]
                if J + 1 < NJ:
                    load6(J + 1)
                for fc in range(32):
                    b = next_bank()
                    for ck in range(8):
                        P.pe((lambda ck, fc, b, hT: lambda e: e.matmul(banks[b][:], lhsT=w1[:, ck, fc * 128:(fc + 1) * 128], rhs=hT[:, ck, :],
                                                                       start=(ck == 0), stop=(ck == 7)))(ck, fc, b, hT), r=['w1_%d' % (fc // 8), 'hT%d_%d' % (p, ck)], w=['pb%d' % b])
                    ri = fc % 2
                    P.act((lambda b, ri: lambda e: e.activation(out=rl[ri], in_=banks[b][:], func=AF.Relu))(b, ri), r=['pb%d' % b], w=['rl%d' % ri])
                    if fc % 2 == 0:
                        P.pool((lambda fc, ri: lambda e: e.tensor_tensor(out=aT[:, fc, :], in0=rl[ri], in1=rl[ri], op=ALU.mult))(fc, ri),
                               r=['rl%d' % ri], w=['aT%d' % fc])
                    else:
                        P.dve((lambda fc, ri: lambda e: e.tensor_tensor(out=aT[:, fc, :], in0=rl[ri], in1=rl[ri], op=ALU.mult))(fc, ri),
                              r=['rl%d' % ri], w=['aT%d' % fc])
                if J + 1 < NJ:
                    stats_norm(xts[1 - p], xsq, rs, rstd, hTs[1 - p], 1 - p, 8, xtok='xt6_%d' % (1 - p))
                for dc in range(8):
                    wi_ = w2_ctr % 2
                    w2_ctr += 1
                    P.dma('sp', 'ld_w2_%d' % wi_, (lambda dc, wi_: lambda e: e.dma_start(out=w2b[wi_], in_=W2[dc]))(dc, wi_), r=['W2'], w=['w2b%d' % wi_])
                    b = next_bank()
                    for fc in range(32):
                        P.pe((lambda fc, b, wi_: lambda e: e.matmul(banks[b][:], lhsT=w2b[wi_][:, fc, :], rhs=aT[:, fc, :],
                                                                    start=(fc == 0), stop=(fc == 31)))(fc, b, wi_), r=['w2b%d' % wi_, 'aT%d' % fc], w=['pb%d' % b])
                    P.dve((lambda dc, b, xt: lambda e: e.tensor_tensor(out=xt[:, dc, :], in0=xt[:, dc, :], in1=banks[b][:], op=ALU.add))(dc, b, xt),
                          r=['pb%d' % b, 'xt6_%d' % p], w=['x2_%d_%d' % (p, dc)])
                for ck in range(8):
                    P.act((lambda ck, xt: lambda e: e.activation(out=xsq[:, ck, :], in_=xt[:, ck, :], func=AF.Square))(ck, xt),
                          r=['x2_%d_%d' % (p, ck)], w=['xsq%d' % ck])
                b = next_bank()
                for ck in range(8):
                    P.pe((lambda ck, b: lambda e: e.matmul(banks[b][:], lhsT=ones[:], rhs=xsq[:, ck, :], start=(ck == 0), stop=(ck == 7)))(ck, b),
                         r=['xsq%d' % ck, 'ones'], w=['pb%d' % b])
                P.act((lambda b: lambda e: e.activation(out=rs, in_=banks[b][:], func=AF.Ln, scale=1.0 / D, bias=EPS))(b), r=['pb%d' % b], w=['rs'])
                P.act(lambda e: e.activation(out=rstd, in_=rs, func=AF.Exp, scale=-0.5), r=['rs'], w=['rstd'])
                for ck in range(8):
                    P.dve((lambda ck, xt: lambda e: e.scalar_tensor_tensor(out=xt[:, ck, :], in0=xt[:, ck, :], scalar=pv[:, 16 + ck:17 + ck], in1=rstd,
                                                                           op0=ALU.mult, op1=ALU.mult))(ck, xt), r=['x2_%d_%d' % (p, ck), 'rstd', 'pv'], w=['x3_%d_%d' % (p, ck)])
                P.dma('sp', 'out%d' % p, (lambda sl, xt: lambda e: e.dma_start(out=outT.rearrange("(c p) t -> p c t", p=128)[:, :, sl], in_=xt))(sl, xt),
                      r=['x3_%d_%d' % (p, ck) for ck in range(8)] + ['xt6_%d' % p], w=['xt6_%d' % p])

        phase6()

        P.emit(nc, final_waits=['out0', 'out1'])
    return nc


_NC_CACHE = {}
_DBG = {}


def _host_consts(c):
    cstv = np.zeros((128, NCST), np.float32)
    cstv[:, 0] = float(c)
    cstv[:, 1] = 1.0 - float(c)
    kk = np.arange(128, dtype=np.float64)
    for h in range(8):
        slope = 2.0 ** (-(h + 1))
        w = SUBW[h]
        for s in range(512 // w):
            q0 = (s + 1) * w - 1
            for delta in range(-56, 8):
                col = 2 + BIAS_BASE[h] + s * 64 + (delta + 56)
                if 128 * delta > 512 * c + q0:
                    cstv[:, col] = -30000.0
                else:
                    cstv[:, col] = (slope * (128 * delta + kk - 512 * c - q0)).astype(np.float32)
    m = np.zeros((128, 8, 512), np.float32)
    kkc = np.arange(128)[:, None]
    qq = np.arange(512)[None, :]
    for i in range(8):
        m[:, i, :] = np.where(128 * i + kkc <= 512 * c + qq, 0.0, -30000.0)
    return cstv, m.astype(ml_dtypes.bfloat16)


def kernel(x, w_in, b_gate, g_mix, lambda_q1, lambda_k1, lambda_q2, lambda_k2, subln_g,
           conv_w, conv_b, w_r, b_r, w_i, b_i, lru_lambda, w_att_out, w_rnn_out, w_o,
           g_mlp, w_ff1, w_ff2, g_final):
    f = lambda a: np.ascontiguousarray(np.asarray(a, dtype=np.float32))
    x = f(x)
    pvec = np.zeros((128, NPV), np.float32)
    col = lambda v: f(v).reshape(-1, 128).T
    pvec[:, 0:8] = col(g_mix[0])
    pvec[:, 8:16] = col(g_mlp[0])
    pvec[:, 16:24] = col(g_final)
    pvec[:, 24:40] = col(b_gate[0])
    for j in range(4):
        pvec[:, 40 + j * 8:48 + j * 8] = col(conv_w[0, j])
    pvec[:, 72:80] = col(conv_b[0])
    pvec[:, 80:88] = col(b_r[0])
    pvec[:, 88:96] = col(b_i[0])
    pvec[:, 96:104] = col(lru_lambda[0])
    pvec[:, 104] = f(subln_g[0])
    pvec[:, 105:169] = f(lambda_q1[0])[None, :]
    pvec[:, 169:233] = f(lambda_k1[0])[None, :]
    pvec[:, 233:297] = f(lambda_q2[0])[None, :]
    pvec[:, 297:361] = f(lambda_k2[0])[None, :]
    shared = {
        "w_in": f(w_in[0]), "w_r": f(w_r[0]), "w_i": f(w_i[0]), "w_att": f(w_att_out[0]), "w_rnn": f(w_rnn_out[0]),
        "w_o": f(w_o[0]), "w_ff1": f(w_ff1[0]), "w_ff2": f(w_ff2[0]), "pvec": pvec,
    }
    consts = [_host_consts(c) for c in range(2)]
    in_maps = []
    for core in range(8):
        p, c = core // 2, core % 2
        xT = np.ascontiguousarray(x[p].T)
        own = xT.reshape(D, 16, T)[:, c::2, :].reshape(D, NOWN)
        d = dict(shared)
        d["xT_seq"] = xT
        d["xT_own"] = np.ascontiguousarray(own)
        d["cst"] = consts[c][0]
        d["amask"] = consts[c][1]
        d["ident_in"] = np.eye(128, dtype=np.float32).astype(ml_dtypes.bfloat16)
        in_maps.append(d)
    if _DBG.get("on"):
        _DBG["in_maps"] = in_maps
        return None
    if "nc" not in _NC_CACHE:
        _NC_CACHE["nc"] = build()
    res = run_bass_kernel_spmd(_NC_CACHE["nc"], in_maps, core_ids=list(range(8)))
    out = np.empty((4, S, D), np.float32)
    for core in range(8):
        p, c = core // 2, core % 2
        oT = np.asarray(res.results[core]["outT"]).reshape(D, 8, T)
        o = oT.transpose(1, 2, 0)
        out[p].reshape(16, T, D)[c::2] = o
    return out
```
